# Optimizing a Trainium2 kernel written in Bass

```python
import jax, jax.numpy as jnp
from jax import lax
import numpy as np

D_MODEL = 1024
BATCH = 4
SEQ = 8192
DEPTH = 1

CHUNK = 64
Q_BLOCK = 128
A_HEADS = 8
A_HEAD_DIM = 64
A_WIDTH = A_HEADS * A_HEAD_DIM
B_HEADS = 8
B_Q_RANK = 256
B_KV_RANK = 128
B_NOPE = 64
B_ROPE = 32
B_V_DIM = 64
B_WIDTH = B_HEADS * B_V_DIM
ROPE_BASE = 10000.0
FFN_HIDDEN = 2816
NORM_EPS = 1e-6
NEG_INF = -1e30
IN_SIZES = (A_WIDTH, A_WIDTH, A_WIDTH, A_HEADS, B_Q_RANK, B_KV_RANK, B_ROPE, D_MODEL, D_MODEL)
IN_WIDTH = 3 * A_WIDTH + A_HEADS + B_Q_RANK + B_KV_RANK + B_ROPE + 2 * D_MODEL

kernel_name = "hybrid_fox_mla_gated_block"


def rms_norm(x, g):
    xf = x.astype(jnp.float32)
    y = xf * lax.rsqrt(jnp.mean(xf * xf, axis=-1, keepdims=True) + NORM_EPS)
    return (y * g.astype(jnp.float32)).astype(x.dtype)


def rope_cos_sin(positions):
    half = B_ROPE // 2
    inv_freq = ROPE_BASE ** (-jnp.arange(half, dtype=jnp.float32) / half)
    ang = positions.astype(jnp.float32)[..., None] * inv_freq
    return jnp.cos(ang), jnp.sin(ang)


def apply_rope(x, cos, sin):
    xf = x.astype(jnp.float32)
    x1, x2 = jnp.split(xf, 2, axis=-1)
    out = jnp.concatenate([x1 * cos - x2 * sin, x2 * cos + x1 * sin], axis=-1)
    return out.astype(x.dtype)


def to_blocks(t):
    b, s = t.shape[0], t.shape[1]
    return t.reshape((b, s // Q_BLOCK, Q_BLOCK) + t.shape[2:]).swapaxes(0, 1)


def from_blocks(t):
    t = t.swapaxes(0, 1)
    return t.reshape((t.shape[0], t.shape[1] * t.shape[2]) + t.shape[3:])


def forgetting_attention(q, k, v, log_f):
    s_len = q.shape[1]
    nb = s_len // Q_BLOCK
    scale = A_HEAD_DIM ** -0.5
    c = jnp.cumsum(log_f, axis=1)
    c_keys = c.transpose(0, 2, 1)
    k_pos = jnp.arange(s_len)

    def one_block(args):
        i, q_i, c_i = args
        q_pos = i * Q_BLOCK + jnp.arange(Q_BLOCK)
        s = jnp.einsum('bqhd,bkhd->bhqk', q_i, k, preferred_element_type=jnp.float32) * scale
        bias = c_i.transpose(0, 2, 1)[..., :, None] - c_keys[..., None, :]
        mask = k_pos[None, :] <= q_pos[:, None]
        s = jnp.where(mask, s + bias, NEG_INF)
        p = jax.nn.softmax(s, axis=-1).astype(v.dtype)
        return jnp.einsum('bhqk,bkhd->bqhd', p, v)

    out = lax.map(one_block, (jnp.arange(nb), to_blocks(q), to_blocks(c)))
    return from_blocks(out)


def latent_attention(q_nope, q_rope, k_nope, k_rope, v):
    s_len = q_nope.shape[1]
    nb = s_len // Q_BLOCK
    scale = (B_NOPE + B_ROPE) ** -0.5
    k_chunk = jnp.arange(s_len) // CHUNK

    def one_block(args):
        i, qn, qr = args
        q_chunk = (i * Q_BLOCK + jnp.arange(Q_BLOCK)) // CHUNK
        s = (jnp.einsum('bqhd,bkhd->bhqk', qn, k_nope, preferred_element_type=jnp.float32)
             + jnp.einsum('bqhr,bkr->bhqk', qr, k_rope, preferred_element_type=jnp.float32))
        s = jnp.where(k_chunk[None, :] <= q_chunk[:, None], s * scale, NEG_INF)
        p = jax.nn.softmax(s, axis=-1).astype(v.dtype)
        return jnp.einsum('bhqk,bkhd->bqhd', p, v)

    out = lax.map(one_block, (jnp.arange(nb), to_blocks(q_nope), to_blocks(q_rope)))
    return from_blocks(out)


def hybrid_mixer(xn, cos, sin, w_in, b_forget, q_a_norm_g, w_q_up, kv_a_norm_g, w_kv_up,
                 w_branch_a, w_branch_b, b_gate, w_out):
    bsz, s_len, _ = xn.shape
    splits = [int(v) for v in np.cumsum(IN_SIZES)[:-1]]
    proj = xn @ w_in
    qa, ka, va, fa, cq, ckv, kr, ga, gb = jnp.split(proj, splits, axis=-1)

    qa = qa.reshape(bsz, s_len, A_HEADS, A_HEAD_DIM)
    ka = ka.reshape(bsz, s_len, A_HEADS, A_HEAD_DIM)
    va = va.reshape(bsz, s_len, A_HEADS, A_HEAD_DIM)
    log_f = jax.nn.log_sigmoid((fa + b_forget).astype(jnp.float32))
    ya = forgetting_attention(qa, ka, va, log_f).reshape(bsz, s_len, A_WIDTH)

    qb = (rms_norm(cq, q_a_norm_g) @ w_q_up).reshape(bsz, s_len, B_HEADS, B_NOPE + B_ROPE)
    q_nope, q_rope = qb[..., :B_NOPE], qb[..., B_NOPE:]
    q_rope = apply_rope(q_rope, cos[:, :, None, :], sin[:, :, None, :])
    kv = (rms_norm(ckv, kv_a_norm_g) @ w_kv_up).reshape(bsz, s_len, B_HEADS, B_NOPE + B_V_DIM)
    k_nope, vb = kv[..., :B_NOPE], kv[..., B_NOPE:]
    k_rope = apply_rope(kr, cos, sin)
    yb = latent_attention(q_nope, q_rope, k_nope, k_rope, vb).reshape(bsz, s_len, B_WIDTH)

    gate_a = jax.nn.sigmoid(ga + b_gate[:D_MODEL])
    gate_b = jax.nn.sigmoid(gb + b_gate[D_MODEL:])
    merged = gate_a * (ya @ w_branch_a) + gate_b * (yb @ w_branch_b)
    return merged @ w_out


def swiglu(xn, w_gate, w_up, w_down):
    return (jax.nn.silu(xn @ w_gate) * (xn @ w_up)) @ w_down


def setup_inputs(seed: int = 0) -> dict:
    key = jax.random.key(seed)
    ks = jax.random.split(key, 20)
    f32 = jnp.float32

    def w(k, shape, fan_in):
        return jax.random.normal(k, shape, f32) * (fan_in ** -0.5)

    def gain(k, shape):
        return 1.0 + 0.05 * jax.random.normal(k, shape, f32)

    x = jax.random.normal(ks[0], (BATCH, SEQ, D_MODEL), f32)
    offsets = jax.random.randint(ks[1], (BATCH, 1), 0, 100000, dtype=jnp.int32)
    positions = offsets + jnp.arange(SEQ, dtype=jnp.int32)[None, :]
    return {
        "x": x,
        "positions": positions,
        "norm_mix_g": gain(ks[2], (DEPTH, D_MODEL)),
        "w_in": w(ks[3], (DEPTH, D_MODEL, IN_WIDTH), D_MODEL),
        "b_forget": jax.random.uniform(ks[4], (DEPTH, A_HEADS), f32, 1.0, 5.0),
        "q_a_norm_g": gain(ks[5], (DEPTH, B_Q_RANK)),
        "w_q_up": w(ks[6], (DEPTH, B_Q_RANK, B_HEADS * (B_NOPE + B_ROPE)), B_Q_RANK),
        "kv_a_norm_g": gain(ks[7], (DEPTH, B_KV_RANK)),
        "w_kv_up": w(ks[8], (DEPTH, B_KV_RANK, B_HEADS * (B_NOPE + B_V_DIM)), B_KV_RANK),
        "w_branch_a": w(ks[9], (DEPTH, A_WIDTH, D_MODEL), A_WIDTH),
        "w_branch_b": w(ks[10], (DEPTH, B_WIDTH, D_MODEL), B_WIDTH),
        "b_gate": 0.02 * jax.random.normal(ks[11], (DEPTH, 2 * D_MODEL), f32),
        "w_out": w(ks[12], (DEPTH, D_MODEL, D_MODEL), D_MODEL),
        "norm_ffn_g": gain(ks[13], (DEPTH, D_MODEL)),
        "w_ffn_gate": w(ks[14], (DEPTH, D_MODEL, FFN_HIDDEN), D_MODEL),
        "w_ffn_up": w(ks[15], (DEPTH, D_MODEL, FFN_HIDDEN), D_MODEL),
        "w_ffn_down": w(ks[16], (DEPTH, FFN_HIDDEN, D_MODEL), FFN_HIDDEN),
        "norm_final_g": gain(ks[17], (D_MODEL,)),
    }


def reference(x, positions, norm_mix_g, w_in, b_forget, q_a_norm_g, w_q_up, kv_a_norm_g,
              w_kv_up, w_branch_a, w_branch_b, b_gate, w_out, norm_ffn_g, w_ffn_gate,
              w_ffn_up, w_ffn_down, norm_final_g):
    cos, sin = rope_cos_sin(positions)
    h = x
    for l in range(DEPTH):
        xn = rms_norm(h, norm_mix_g[l])
        h = h + hybrid_mixer(xn, cos, sin, w_in[l], b_forget[l], q_a_norm_g[l], w_q_up[l],
                             kv_a_norm_g[l], w_kv_up[l], w_branch_a[l], w_branch_b[l],
                             b_gate[l], w_out[l])
        hn = rms_norm(h, norm_ffn_g[l])
        h = h + swiglu(hn, w_ffn_gate[l], w_ffn_up[l], w_ffn_down[l])
    return rms_norm(h, norm_final_g)
```

```python
from contextlib import ExitStack
import numpy as np
import ml_dtypes
FEAT = set('ka,fa,va,bs,bs2,bs3,bs4,kvup,own,qa,gate,cq,scan'.split(','))
import concourse.bass as bass
import concourse.mybir as mybir
from concourse.bass_utils import run_bass_kernel_spmd

F32 = mybir.dt.float32
BF16 = mybir.dt.bfloat16
I32 = mybir.dt.int32
AF = mybir.ActivationFunctionType
ALU = mybir.AluOpType
AX = mybir.AxisListType

S = 8192
SO = 4096
D = 1024
NH = 8
FF = 2816
NFC = FF // 128
EPS = 1e-6
OWN = ([0, 3, 5, 6], [1, 2, 4, 7])
NEG = -30000.0
O_QA, O_KA, O_VA, O_FA, O_CQ, O_CKV, O_KR, O_GA, O_GB = 0, 512, 1024, 1536, 1544, 1800, 1928, 1960, 2984
INW = 4008
SC_A = 64 ** -0.5
SC_B = 96 ** -0.5
TWO_PI = 2.0 * np.pi
CW1 = 6.28125
CW2 = float(np.float32(TWO_PI - 6.28125))


class Buf:
    __slots__ = ("name", "t", "w", "r", "ch", "chv", "excl")

    def __init__(self, name, t=None, excl=False):
        self.name = name
        self.excl = excl
        self.t = t
        self.w = None
        self.r = {}
        self.ch = None
        self.chv = 0

    def __getitem__(self, k):
        return self.t[k]


class Sched:
    def __init__(self, nc):
        self.nc = nc
        self.eng = dict(pe=nc.tensor, act=nc.scalar, dve=nc.vector, pool=nc.gpsimd, sp=nc.sync)
        self.sems = {}
        self.cnt = {}
        for k in ("pe", "act", "dve", "pool"):
            self.sems["sem_" + k] = nc.alloc_semaphore("sem_" + k)
            self.cnt[k] = 0
        self.known = {k: {} for k in self.eng}
        self.chans = []
        self.pe_pending = False
        self.nwait = 0

    def _deps(self, e, R, W, acc):
        deps = []
        own = "sem_" + e
        for b in R:
            if b.w is not None:
                deps.append(b.w)
            if b.excl:
                deps.extend((k, v) for k, v in b.r.items() if k != own)
        if not acc:
            for b in W:
                if b.w is not None:
                    deps.append(b.w)
                deps.extend(b.r.items())
        kn = self.known[e]
        for sn, val in deps:
            if e == "pe" and sn == "sem_pe":
                continue
            if kn.get(sn, 0) >= val:
                continue
            self.eng[e].wait_ge(self.sems[sn], val)
            self.nwait += 1
            kn[sn] = val

    def _mark(self, ev, R, W):
        for b in R:
            if b.r.get(ev[0], 0) < ev[1]:
                b.r[ev[0]] = ev[1]
        for b in W:
            b.w = ev
            b.r = {}

    def op(self, e, fn, R=(), W=(), acc=False, signal=True):
        self._deps(e, R, W, acc)
        inst = fn()
        sn = "sem_" + e
        if signal:
            self.cnt[e] += 1
            inst.then_inc(self.sems[sn], 1)
            ev = (sn, self.cnt[e])
            if e == "pe":
                self.pe_pending = False
        else:
            assert e == "pe"
            ev = (sn, self.cnt[e] + 1)
            self.pe_pending = True
        self._mark(ev, R, W)

    def dma(self, q, out, in_, R=(), W=(), ch=None, group=False, **kw):
        self._deps(q, R, W, group)
        if ch.ch is None:
            ch.ch = "ch%d" % len(self.chans)
            self.sems[ch.ch] = self.nc.alloc_semaphore(ch.ch)
            self.chans.append(ch)
        inst = self.eng[q].dma_start(out=out, in_=in_, **kw)
        ch.chv += 16
        inst.then_inc(self.sems[ch.ch], 16)
        self._mark((ch.ch, ch.chv), R, W)

    def barrier(self):
        assert not self.pe_pending
        evs = [("sem_" + k, v) for k, v in self.cnt.items() if v > 0]
        evs += [(b.ch, b.chv) for b in self.chans if b.chv > 0]
        for e in self.eng:
            kn = self.known[e]
            for sn, val in evs:
                if kn.get(sn, 0) >= val:
                    continue
                self.eng[e].wait_ge(self.sems[sn], val)
                kn[sn] = val


def build_program(debug=False, stop_after=9):
    nc = bass.Bass("TRN2", target_bir_lowering=False)
    sc = Sched(nc)
    op = sc.op
    dma = sc.dma

    def din(name, shape, dt=F32):
        return nc.dram_tensor(name, list(shape), dt, kind="ExternalInput").ap()

    def dscr(name, shape, dt):
        return nc.dram_tensor(name, list(shape), dt, kind="ExternalOutput" if debug else "Internal").ap()

    x_all = din("x_all", [S, D])
    x_own = din("x_own", [SO, D])
    pos_all = din("pos_all", [128, 64], I32)
    pos_own = din("pos_own", [128, 32], I32)
    w_in = din("w_in", [D, INW])
    w_q = din("w_q", [256, 768])
    w_kv = din("w_kv", [128, 1024])
    w_a = din("w_a", [512, D])
    w_b = din("w_b", [512, D])
    w_o = din("w_o", [D, D])
    w_g = din("w_g", [D, FF])
    w_u = din("w_u", [D, FF])
    w_d = din("w_d", [FF, D])
    g_mix = din("g_mix", [128, 8])
    g_ffn = din("g_ffn", [128, 8])
    g_q = din("g_q", [128, 2])
    g_kv = din("g_kv", [128, 1])
    g_fin = din("g_fin", [1, D])
    g_ffn_row = din("g_ffn_row", [1, D])
    b_f = din("b_f", [8, 1])
    b_gate = din("b_gate", [128, 16])
    ident_in = din("ident", [128, 128])
    invf_in = din("invf", [128, 16])
    mask_a_in = din("mask_a", [128, 8 * 512])
    mask_b_in = din("mask_b", [128, 8 * 512])
    sel_in = din("sel", [8, 8])
    out_own = nc.dram_tensor("out_own", [SO, D], F32, kind="ExternalOutput").ap()

    KA = dscr("scr_ka", [NH, 68, S], BF16)
    KBN = dscr("scr_kbn", [NH, 64, S], BF16)
    KR = dscr("scr_kr", [32, S], BF16)
    VA = dscr("scr_va", [S, 512], BF16)
    VB = dscr("scr_vb", [S, 512], BF16)
    QA = dscr("scr_qa", [NH, 68, SO], BF16)
    QB = dscr("scr_qb", [NH, 96, SO], BF16)
    GT = dscr("scr_gt", [16, 128, 16, 256], BF16)
    YT = dscr("scr_yt", [16, 128, 8, 256], BF16)
    HH = dscr("scr_h", [SO, D], F32)
    HNT = dscr("scr_hnt", [16, 128, 8 * 256], BF16)

    with ExitStack() as top:
        def sbt(es, name, shape, dt, side=None):
            return Buf(name, es.enter_context(nc.sbuf_tensor("sb_" + name, list(shape), dt, side=side)))

        def pst(es, name, shape, dt):
            return Buf(name, es.enter_context(nc.psum_tensor("ps_" + name, list(shape), dt)), excl=True)

        ident = sbt(top, "ident", [128, 128], BF16)
        identf = sbt(top, "identf", [128, 128], F32)
        dma("sp", identf[:], ident_in, W=[identf], ch=identf)
        op("dve", lambda: nc.vector.tensor_copy(out=ident[:], in_=identf[:]), R=[identf], W=[ident])

        def emit_rstd(ss, lnv, rstd, n, inv_n, P=128):
            op("act", lambda: nc.scalar.activation(out=lnv[0:P, 0:n], in_=ss[0:P, 0:n], func=AF.Ln,
                                                    bias=epsb[0:P, 0:1], scale=inv_n),
               R=[ss, epsb], W=[lnv])
            op("act", lambda: nc.scalar.activation(out=rstd[0:P, 0:n], in_=lnv[0:P, 0:n], func=AF.Exp,
                                                    scale=-0.5),
               R=[lnv], W=[rstd])

        epsb = sbt(top, "epsb", [128, 1], F32)
        op("dve", lambda: nc.vector.memset(epsb[:], EPS), W=[epsb])

        def load_weight(es_stage, dst, dst_ap_fn, src_ap_fn, nchunks, width, gain=None, gain_col=None,
                        stage_ring=None, eng_cycle=("dve", "pool")):
            for c in range(nchunks):
                st = stage_ring[c % len(stage_ring)]
                dma("sp", st[:, 0:width], src_ap_fn(c), W=[st], ch=st)
                e = eng_cycle[c % len(eng_cycle)]
                E = nc.vector if e == "dve" else nc.gpsimd
                if gain is not None:
                    gc = gain_col(c)
                    op(e, lambda E=E, st=st, c=c, gc=gc: E.tensor_scalar(
                        out=dst_ap_fn(c), in0=st[:, 0:width], scalar1=gain[:, gc:gc + 1], scalar2=1.0,
                        op0=ALU.mult, op1=ALU.mult), R=[st, gain], W=[dst])
                else:
                    op(e, lambda E=E, st=st, c=c: E.tensor_copy(out=dst_ap_fn(c), in_=st[:, 0:width]),
                       R=[st], W=[dst])

        def load_weight_cast(dst, dst_ap_fn, src_ap_fn, nchunks):
            for c in range(nchunks):
                dma("pool", dst_ap_fn(c), src_ap_fn(c), W=[dst], ch=dst, group=(c > 0), max_dma_last_dim=2048)

        oneb = sbt(top, "oneb", [128, 1], F32)
        op("dve", lambda: nc.vector.memset(oneb[:], 1.0), W=[oneb])

        class Ring:
            def __init__(self, es, name, n, shape, dt, psum=False):
                mk = pst if psum else sbt
                self.b = [mk(es, "%s%d" % (name, i), shape, dt) for i in range(n)]
                self.i = 0

            def next(self):
                b = self.b[self.i % len(self.b)]
                self.i += 1
                return b

        with ExitStack() as p1:
            Win = sbt(p1, "Win", [128, 8 * INW], BF16)
            Wkv = sbt(p1, "Wkv", [128, 1024], BF16)
            Wq = sbt(p1, "Wq", [128, 2 * 768], BF16)
            gmix = sbt(p1, "gmix", [128, 8], F32)
            gq = sbt(p1, "gq", [128, 2], F32)
            gkv = sbt(p1, "gkv", [128, 1], F32)
            bg = sbt(p1, "bg", [128, 16], F32)
            negb = sbt(p1, "negb", [8, 1], F32)
            bfr = sbt(p1, "bfr", [8, 1], F32)
            sel = sbt(p1, "sel", [8, 8], F32)
            invf = sbt(p1, "invf", [128, 16], F32)
            cos_all = sbt(p1, "cos_all", [128, 64 * 16], F32)
            sin_all = sbt(p1, "sin_all", [128, 64 * 16], F32)
            cos_own = sbt(p1, "cos_own", [128, 32 * 16], F32)
            sin_own = sbt(p1, "sin_own", [128, 32 * 16], F32)
            LH = sbt(p1, "LH", [8, S], BF16)
            ones8 = sbt(p1, "ones8", [8, 512], BF16)
            op("pool", lambda: nc.gpsimd.memset(ones8[:], 1.0), W=[ones8])
            for (dst, src) in ((gmix, g_mix), (gq, g_q), (gkv, g_kv), (bg, b_gate), (bfr, b_f), (sel, sel_in),
                               (invf, invf_in)):
                dma("sp", dst[:], src, W=[dst], ch=dst)
            op("dve", lambda: nc.vector.tensor_scalar(out=negb[:], in0=bfr[:], scalar1=-1.0, scalar2=None,
                                                      op0=ALU.mult), R=[bfr], W=[negb])

            xblk = Ring(p1, "xblk", 6, [128, D], F32)
            sqr = Ring(p1, "sqr", 2, [128, D], BF16)
            xsr = Ring(p1, "xsr", 5, [128, D], BF16)
            ssr = Ring(p1, "ssr", 4, [128, 1], F32)
            lnr = Ring(p1, "lnr", 4, [128, 1], F32)
            rsr = Ring(p1, "rsr", 4, [128, 1], F32)
            ss4r = Ring(p1, "ss4r", 3, [128, 4], F32)
            ln4r = Ring(p1, "ln4r", 3, [128, 4], F32)
            rs4r = Ring(p1, "rs4r", 3, [128, 4], F32)
            def prepL(xsrc, T):
                xbs = []
                for j in range(4):
                    xb = xblk.next()
                    t0 = T * 512 + j * 128
                    dma("pool", xb[:], xsrc[t0:t0 + 128, :], W=[xb], ch=xb)
                    xbs.append(xb)
                return xbs

            def prepA_sq(xbs, j, st3):
                xb = xbs[j]
                sq = sqr.next()
                op("act", lambda: nc.scalar.activation(out=sq[:], in_=xb[:], func=AF.Square), R=[xb], W=[sq])
                op("dve", lambda: nc.vector.tensor_reduce(out=st3[0][:, j:j + 1], in_=sq[:], axis=AX.X, op=ALU.add),
                   R=[sq], W=[st3[0]])

            def prepA_fin(xbs, st3):
                ss4, ln4, rs4 = st3
                emit_rstd(ss4, ln4, rs4, 4, 1.0 / D)
                xss = []
                for j in range(4):
                    xs = xsr.next()
                    xb = xbs[j]
                    op("dve", lambda: nc.vector.tensor_scalar(out=xs[:], in0=xb[:], scalar1=rs4[:, j:j + 1],
                                                              scalar2=None, op0=ALU.mult), R=[xb, rs4], W=[xs])
                    xss.append(xs)
                return xss

            def prepA(xbs):
                st3 = (ss4r.next(), ln4r.next(), rs4r.next())
                for j in range(4):
                    prepA_sq(xbs, j, st3)
                return prepA_fin(xbs, st3)

            jobs = [("all", T) for T in range(16)] + [("own", T) for T in range(8)]
            if stop_after == 0:
                jobs = jobs[:1] + (jobs[16:17] if 'own' in FEAT else [])
            def do_prepL(job):
                return prepL(x_all if job[0] == "all" else x_own, job[1])

            NJ = len(jobs)
            xL = {0: do_prepL(jobs[0])}
            xsA = {0: prepA(xL.pop(0))}
            if NJ > 1:
                xL[1] = do_prepL(jobs[1])

            with ExitStack() as tmp:
                def rope_tables(pos_src, nb, cos_t, sin_t, tag):
                    n = nb * 16
                    pi_ = sbt(tmp, "pi_" + tag, [128, nb], I32)
                    pf = sbt(tmp, "pf_" + tag, [128, nb], F32)
                    ang = sbt(tmp, "ang_" + tag, [128, n], F32)
                    t1 = sbt(tmp, "t1_" + tag, [128, n], F32)
                    ki = sbt(tmp, "ki_" + tag, [128, n], I32)
                    dma("sp", pi_[:], pos_src, W=[pi_], ch=pi_)
                    op("dve", lambda: nc.vector.tensor_copy(out=pf[:], in_=pi_[:]), R=[pi_], W=[pf])
                    a3 = ang[:].rearrange("p (b j) -> p b j", j=16)
                    op("dve", lambda: nc.vector.tensor_tensor(
                        out=a3, in0=pf[:].unsqueeze(2).broadcast_to([128, nb, 16]),
                        in1=invf[:].unsqueeze(1).broadcast_to([128, nb, 16]), op=ALU.mult),
                       R=[pf, invf], W=[ang])
                    op("dve", lambda: nc.vector.tensor_scalar(out=t1[:], in0=ang[:], scalar1=float(1.0 / TWO_PI),
                                                              scalar2=None, op0=ALU.mult), R=[ang], W=[t1])
                    op("dve", lambda: nc.vector.tensor_copy(out=ki[:], in_=t1[:]), R=[t1], W=[ki])
                    op("dve", lambda: nc.vector.tensor_copy(out=t1[:], in_=ki[:]), R=[ki], W=[t1])
                    op("dve", lambda: nc.vector.scalar_tensor_tensor(out=ang[:], in0=t1[:], scalar=-CW1, in1=ang[:],
                                                                     op0=ALU.mult, op1=ALU.add),
                       R=[t1, ang], W=[ang])
                    op("dve", lambda: nc.vector.scalar_tensor_tensor(out=ang[:], in0=t1[:], scalar=-CW2, in1=ang[:],
                                                                     op0=ALU.mult, op1=ALU.add),
                       R=[t1, ang], W=[ang])
                    PI_ = float(np.pi)
                    op("dve", lambda: nc.vector.tensor_single_scalar(out=t1[:], in_=ang[:], scalar=PI_, op=ALU.is_gt),
                       R=[ang], W=[t1])
                    op("dve", lambda: nc.vector.scalar_tensor_tensor(out=ang[:], in0=t1[:], scalar=-float(TWO_PI),
                                                                     in1=ang[:], op0=ALU.mult, op1=ALU.add),
                       R=[t1, ang], W=[ang])
                    op("dve", lambda: nc.vector.tensor_single_scalar(out=t1[:], in_=ang[:], scalar=-PI_, op=ALU.is_lt),
                       R=[ang], W=[t1])
                    op("dve", lambda: nc.vector.scalar_tensor_tensor(out=ang[:], in0=t1[:], scalar=float(TWO_PI),
                                                                     in1=ang[:], op0=ALU.mult, op1=ALU.add),
                       R=[t1, ang], W=[ang])
                    PC = 3.1415925
                    op("dve", lambda: nc.vector.tensor_scalar(out=ang[:], in0=ang[:], scalar1=PC, scalar2=-PC,
                                                              op0=ALU.min, op1=ALU.max), R=[ang], W=[ang])
                    op("act", lambda: nc.scalar.activation(out=sin_t[:], in_=ang[:], func=AF.Sin),
                       R=[ang], W=[sin_t])
                    op("dve", lambda: nc.vector.tensor_scalar(out=t1[:], in0=ang[:], scalar1=-1.0, scalar2=None,
                                                              op0=ALU.mult), R=[ang], W=[t1])
                    op("dve", lambda: nc.vector.tensor_tensor(out=t1[:], in0=t1[:], in1=ang[:], op=ALU.max),
                       R=[ang, t1], W=[t1])
                    op("dve", lambda: nc.vector.tensor_scalar(out=t1[:], in0=t1[:], scalar1=-1.0,
                                                              scalar2=float(np.pi / 2), op0=ALU.mult, op1=ALU.add),
                       R=[t1], W=[t1])
                    op("act", lambda: nc.scalar.activation(out=cos_t[:], in_=t1[:], func=AF.Sin),
                       R=[t1], W=[cos_t])

                rope_tables(pos_all, 64, cos_all, sin_all, "a")
                rope_tables(pos_own, 32, cos_own, sin_own, "o")
                wst = [sbt(tmp, "wst%d" % i, [128, 2048], F32) for i in range(2)]
                HW = INW // 2
                load_weight(tmp, Win,
                            lambda c: Win[:, (c // 2) * INW + (c % 2) * HW:(c // 2) * INW + (c % 2) * HW + HW],
                            lambda c: w_in[(c // 2) * 128:(c // 2) * 128 + 128, (c % 2) * HW:(c % 2) * HW + HW],
                            16, HW, gain=gmix, gain_col=lambda c: c // 2, stage_ring=wst)
                load_weight(tmp, Wkv, lambda c: Wkv[:, :], lambda c: w_kv, 1, 1024, gain=gkv,
                            gain_col=lambda c: 0, stage_ring=wst)
                load_weight(tmp, Wq, lambda c: Wq[:, c * 768:(c + 1) * 768],
                            lambda c: w_q[c * 128:(c + 1) * 128, :],
                            2, 768, gain=gq, gain_col=lambda c: c, stage_ring=wst)

                sc.barrier()
                if stop_after == 0.1:
                    dbg = nc.dram_tensor("dbg", [128, 1024], F32, kind="ExternalOutput").ap()
                    dma("sp", dbg, cos_all[:], R=[cos_all], ch=cos_all)
                    dbg2 = nc.dram_tensor("dbg2", [128, 1024], F32, kind="ExternalOutput").ap()
                    dma("sp", dbg2, sin_all[:], R=[sin_all], ch=sin_all)
                    dbg3 = nc.dram_tensor("dbg3", [128, 8 * INW], BF16, kind="ExternalOutput").ap()
                    dma("sp", dbg3, Win[:], R=[Win], ch=Win)
                    sc.barrier()
                    return nc

            xnTr = Ring(p1, "xnT", 2, [128, 8 * 512], BF16)
            stg = Ring(p1, "stg", 4, [128, 512], BF16)
            sqB = Ring(p1, "sqB", 2, [128, 256], BF16)
            ckvn = Ring(p1, "ckvn", 3, [128, 128], BF16)
            krr = Ring(p1, "krr", 3, [128, 128], BF16)
            for _b in krr.b:
                op("pool", lambda: nc.gpsimd.memset(_b[:], 0.0), W=[_b])
            rtk = Ring(p1, "rtk", 3, [128, 64], F32)
            ckvnT = Ring(p1, "ckvnT", 2, [128, 512], BF16)
            krT = Ring(p1, "krT", 2, [128, 512], BF16)
            cqn = Ring(p1, "cqn", 3, [128, 256], BF16)
            cqnT = Ring(p1, "cqnT", 3, [128, 256], BF16)
            qb = Ring(p1, "qb", 3, [128, 768], BF16)
            rtq = Ring(p1, "rtq", 4, [128, 2 * 128], F32)
            qbst = Ring(p1, "qbst", 2, [128, 1024], BF16)
            tmpf = Ring(p1, "tmpf", 2, [8, 512], F32)
            lTr = Ring(p1, "lTr", 2, [8, 512], F32)
            l_mid = Ring(p1, "l_mid", 2, [8, 512], BF16)
            l_lo = Ring(p1, "l_lo", 2, [8, 512], BF16)
            l_r = Ring(p1, "l_r", 2, [8, 512], F32)
            eights = sbt(p1, "eights", [8, 512], BF16)
            op("pool", lambda: nc.gpsimd.memset(eights[:], 8.0), W=[eights])
            ptx = Ring(p1, "ptx", 2, [128, 1024], BF16, psum=True)
            ppj = Ring(p1, "ppj", 2, [128, 512], F32, psum=True)
            pB = Ring(p1, "pB", 3, [128, 512], F32, psum=True)
            pBt = pst(p1, "pBt", [128, 1024], BF16)

            evac_flip = [0]
            evac_pat = [(1, 1, 1, 0)]
            lt_prev = [None]

            def evac(out_ap, in_ap, R, W):
                pat = evac_pat[0]
                evac_flip[0] = (evac_flip[0] + 1) % len(pat)
                if pat[evac_flip[0]]:
                    op("act", lambda: nc.scalar.copy(out=out_ap, in_=in_ap), R=R, W=W)
                else:
                    op("dve", lambda: nc.vector.tensor_copy(out=out_ap, in_=in_ap), R=R, W=W)

            def norm_block(src_ap, R_src, n, sq_buf, nparts=128):
                ss, ln_, rs = ssr.next(), lnr.next(), rsr.next()
                return ss, ln_, rs

            def prepB(xss):
                xn = xnTr.next()
                for j in range(4):
                    xs = xss[j]
                    pt = ptx.next()
                    for kc in range(8):
                        op("pe", lambda: nc.tensor.transpose(out=pt[:, kc * 128:(kc + 1) * 128],
                                                             in_=xs[:, kc * 128:(kc + 1) * 128], identity=ident[:]),
                           R=[xs, ident], W=[pt], acc=(kc > 0), signal=(kc == 7))
                    xv = xn[:].rearrange("p (c t) -> p c t", c=8)[:, :, j * 128:(j + 1) * 128]
                    pv = pt[:].rearrange("p (c t) -> p c t", c=8)
                    evac(xv, pv, [pt], [xn])
                return xn

            def proj_fm(xn, col0, ncols, N=512):
                ps = ppj.next()
                for kc in range(8):
                    op("pe", lambda: nc.tensor.matmul(ps[0:ncols, 0:N], lhsT=Win[:, kc * INW + col0:kc * INW + col0 + ncols],
                                                      rhs=xn[:, kc * 512:kc * 512 + N], start=(kc == 0), stop=(kc == 7)),
                       R=[Win, xn], W=[ps], acc=(kc > 0), signal=(kc == 7))
                return ps

            def proj_tm(xn, j, col0, ncols, ring):
                ps = ring.next()
                for kc in range(8):
                    op("pe", lambda: nc.tensor.matmul(ps[:, 0:ncols], lhsT=xn[:, kc * 512 + j * 128:kc * 512 + j * 128 + 128],
                                                      rhs=Win[:, kc * INW + col0:kc * INW + col0 + ncols],
                                                      start=(kc == 0), stop=(kc == 7)),
                       R=[Win, xn], W=[ps], acc=(kc > 0), signal=(kc == 7))
                return ps

            def small_norm(ps, c0, n, out_bf):
                sq = sqB.next()
                op("act", lambda: nc.scalar.activation(out=sq[:, 0:n], in_=ps[:, c0:c0 + n], func=AF.Square),
                   R=[ps], W=[sq])
                ss, ln_, rs = ssr.next(), lnr.next(), rsr.next()
                op("dve", lambda: nc.vector.tensor_reduce(out=ss[:], in_=sq[:, 0:n], axis=AX.X, op=ALU.add),
                   R=[sq], W=[ss])
                emit_rstd(ss, ln_, rs, 1, 1.0 / n)
                op("dve", lambda: nc.vector.tensor_scalar(out=out_bf[:, 0:n], in0=ps[:, c0:c0 + n],
                                                          scalar1=rs[:, 0:1], scalar2=None, op0=ALU.mult),
                   R=[ps, rs], W=[out_bf])

            pend_kv = []

            def proj_all(xn, T, hook1, hA, flush_kv=None):
                t0 = T * 512
                ckT = ckvnT.next()
                krt = krT.next()
                cns, krs = {}, {}

                def s1(j):
                    ps = proj_tm(xn, j, O_VA, 512, ppj)
                    st = stg.next()
                    evac(st[:], ps[:], [ps], [st])
                    dma("sp", VA[t0 + j * 128:t0 + j * 128 + 128, :], st[:], R=[st], ch=st)
                    pb = proj_tm(xn, j, O_CKV, 160, pB)
                    cn = ckvn.next()
                    small_norm(pb, 0, 128, cn)
                    blk = T * 4 + j
                    rt = rtk.next()
                    kr_ = krr.next()
                    cb = cos_all[:, blk * 16:blk * 16 + 16].unsqueeze(1).broadcast_to([128, 2, 16])
                    sb_ = sin_all[:, blk * 16:blk * 16 + 16].unsqueeze(1).broadcast_to([128, 2, 16])
                    kv3 = pb[:, 128:160].rearrange("p (a j) -> p a j", a=2)
                    op("dve", lambda: nc.vector.tensor_tensor(out=rt[:, 0:32].rearrange("p (a j) -> p a j", a=2),
                                                              in0=kv3, in1=cb, op=ALU.mult),
                       R=[pb, cos_all], W=[rt])
                    op("dve", lambda: nc.vector.tensor_tensor(out=rt[:, 32:64].rearrange("p (a j) -> p a j", a=2),
                                                              in0=kv3, in1=sb_, op=ALU.mult),
                       R=[pb, sin_all], W=[rt])
                    op("dve", lambda: nc.vector.tensor_tensor(out=kr_[:, 0:16], in0=rt[:, 0:16], in1=rt[:, 48:64],
                                                              op=ALU.subtract), R=[rt], W=[kr_])
                    op("dve", lambda: nc.vector.tensor_tensor(out=kr_[:, 16:32], in0=rt[:, 16:32], in1=rt[:, 32:48],
                                                              op=ALU.add), R=[rt], W=[kr_])
                    cns[j], krs[j] = cn, kr_

                def s2(j):
                    cn, kr_ = cns.pop(j), krs.pop(j)
                    op("pe", lambda: nc.tensor.transpose(out=pBt[:, 0:128], in_=cn[:, 0:128], identity=ident[:]),
                       R=[cn, ident], W=[pBt], signal=False)
                    op("pe", lambda: nc.tensor.transpose(out=pBt[:, 128:256], in_=kr_[:, 0:128], identity=ident[:]),
                       R=[kr_, ident], W=[pBt], acc=True)
                    pv2 = pBt[:, 0:256].rearrange("p (a t) -> p a t", a=2)
                    evac(ckT[:, j * 128:(j + 1) * 128], pBt[:, 0:128], [pBt], [ckT])
                    evac(krt[0:32, j * 128:(j + 1) * 128], pBt[0:32, 128:256], [pBt], [krt])

                def ka(jj):
                    ps = proj_fm(xn, O_KA + jj * 128, 128)
                    st = stg.next()
                    evac(st[:], ps[:], [ps], [st])
                    for hh_ in range(2):
                        dma("sp", KA[2 * jj + hh_, 0:64, t0:t0 + 512], st[hh_ * 64:(hh_ + 1) * 64, :], R=[st], ch=st)

                def forget():
                    ps = proj_fm(xn, O_FA, 8)
                    tf = tmpf.next()
                    op("act", lambda: nc.scalar.activation(out=tf[:], in_=ps[0:8, :], func=AF.Exp, bias=negb[:, 0:1],
                                                           scale=-1.0), R=[ps, negb], W=[tf])
                    lt = lTr.next()
                    op("act", lambda: nc.scalar.activation(out=lt[:], in_=tf[:], func=AF.Ln,
                                                           bias=oneb[0:8, 0:1], scale=1.0), R=[tf, oneb], W=[lt])
                    init = 0.0 if T == 0 else lt_prev[0][:, 511:512]
                    op("dve", lambda: nc.vector.tensor_tensor_scan(out=lt[:], data0=ones8[:], data1=lt[:], initial=init,
                                                                   op0=ALU.mult, op1=ALU.add),
                       R=[lt, ones8] + ([lt_prev[0]] if T > 0 else []), W=[lt])
                    lt_prev[0] = lt
                    lm, ll, lr = l_mid.next(), l_lo.next(), l_r.next()
                    lhs_ = LH[:, t0:t0 + 512]
                    op("dve", lambda: nc.vector.tensor_copy(out=lhs_, in_=lt[:]), R=[lt], W=[LH])
                    op("dve", lambda: nc.vector.tensor_tensor(out=lr[:], in0=lt[:], in1=lhs_, op=ALU.subtract),
                       R=[lt, LH], W=[lr])
                    op("dve", lambda: nc.vector.tensor_copy(out=lm[:], in_=lr[:]), R=[lr], W=[lm])
                    op("dve", lambda: nc.vector.tensor_tensor(out=lr[:], in0=lr[:], in1=lm[:], op=ALU.subtract),
                       R=[lr, lm], W=[lr])
                    op("dve", lambda: nc.vector.tensor_copy(out=ll[:], in_=lr[:]), R=[lr], W=[ll])
                    dma("sp", KA[:, 64, t0:t0 + 512], eights[:], R=[eights], ch=eights)
                    dma("sp", KA[:, 65, t0:t0 + 512], lhs_, R=[LH], ch=LH)
                    dma("sp", KA[:, 66, t0:t0 + 512], lm[:], R=[lm], ch=lm)
                    dma("sp", KA[:, 67, t0:t0 + 512], ll[:], R=[ll], ch=ll)

                def kvk(pT, pckT, jj):
                    pt0 = pT * 512
                    ps = ppj.next()
                    op("pe", lambda: nc.tensor.matmul(ps[:, :], lhsT=Wkv[:, jj * 128:(jj + 1) * 128], rhs=pckT[:, :],
                                                      start=True, stop=True), R=[Wkv, pckT], W=[ps])
                    st = stg.next()
                    evac(st[:], ps[:], [ps], [st])
                    dma("sp", KBN[2 * jj:2 * jj + 2, :, pt0:pt0 + 512].rearrange("h r t -> (h r) t"), st[:],
                        R=[st], ch=st)

                def vbk(pT, pckT, j):
                    pt0 = pT * 512
                    ps = ppj.next()
                    op("pe", lambda: nc.tensor.matmul(ps[:, :], lhsT=pckT[:, j * 128:(j + 1) * 128], rhs=Wkv[:, 512:1024],
                                                      start=True, stop=True), R=[Wkv, pckT], W=[ps])
                    st = stg.next()
                    evac(st[:], ps[:], [ps], [st])
                    dma("sp", VB[pt0 + j * 128:pt0 + j * 128 + 128, :], st[:], R=[st], ch=st)

                prev = pend_kv.pop() if pend_kv else None

                def pk(jj):
                    if prev is not None:
                        kvk(prev[0], prev[1], jj)

                def pv_(j):
                    if prev is not None:
                        vbk(prev[0], prev[1], j)

                s1(0)
                pk(0)
                s1(1)
                pk(1)
                s2(0)
                hook1()
                s1(2)
                pk(2)
                s2(1)
                s1(3)
                pk(3)
                s2(2)
                ka(0)
                pv_(0)
                hA[0]()
                ka(1)
                pv_(1)
                hA[1]()
                s2(3)
                dma("sp", KR[:, t0:t0 + 512], krt[0:32, :], R=[krt], ch=krt)
                ka(2)
                pv_(2)
                hA[2]()
                ka(3)
                pv_(3)
                hA[3]()
                forget()
                hA[4]()
                pend_kv.append((T, ckT))
                if flush_kv is not None:
                    for jj in range(4):
                        kvk(T, ckT, jj)
                    for j in range(4):
                        vbk(T, ckT, j)
                    pend_kv.pop()

            def proj_own(xn, T, hook1, hA):
                t0 = T * 512
                cqs, qbs = {}, {}

                def qa(jj):
                    ps = proj_fm(xn, O_QA + jj * 128, 128)
                    st = stg.next()
                    evac(st[:], ps[:], [ps], [st])
                    for hh_ in range(2):
                        dma("sp", QA[2 * jj + hh_, 0:64, t0:t0 + 512], st[hh_ * 64:(hh_ + 1) * 64, :], R=[st], ch=st)

                def gate(gc):
                    ps = proj_fm(xn, O_GA + gc * 128, 128)
                    st = stg.next()
                    op("act", lambda: nc.scalar.activation(out=st[:], in_=ps[:], func=AF.Sigmoid,
                                                           bias=bg[:, gc:gc + 1], scale=1.0), R=[ps, bg], W=[st])
                    for uu in range(2):
                        dma("sp", GT[2 * T + uu, :, gc, :], st[:, uu * 256:(uu + 1) * 256], R=[st], ch=st)

                def o1(j):
                    pb = proj_tm(xn, j, O_CQ, 256, pB)
                    cq_ = cqn.next()
                    small_norm(pb, 0, 256, cq_)
                    cqs[j] = cq_

                def o2(j):
                    blk = T * 4 + j
                    cq_ = cqs.pop(j)
                    cT = cqnT.next()
                    op("pe", lambda: nc.tensor.transpose(out=pBt[:, 0:128], in_=cq_[:, 0:128], identity=ident[:]),
                       R=[cq_, ident], W=[pBt], signal=False)
                    op("pe", lambda: nc.tensor.transpose(out=pBt[:, 128:256], in_=cq_[:, 128:256], identity=ident[:]),
                       R=[cq_, ident], W=[pBt], acc=True)
                    evac(cT[:, 0:256], pBt[:, 0:256], [pBt], [cT])
                    qb_ = qb.next()
                    q3 = qb_[:].rearrange("p (h c) -> p h c", h=8)
                    for half in range(2):
                        pq = pB.next()
                        for lc in range(2):
                            op("pe", lambda: nc.tensor.matmul(pq[:, 0:384], lhsT=cT[:, lc * 128:(lc + 1) * 128],
                                                              rhs=Wq[:, lc * 768 + half * 384:lc * 768 + half * 384 + 384],
                                                              start=(lc == 0), stop=(lc == 1)),
                               R=[Wq, cT], W=[pq], acc=(lc > 0), signal=(lc == 1))
                        p3 = pq[:, 0:384].rearrange("p (h c) -> p h c", h=4)
                        qh = q3[:, half * 4:(half + 1) * 4, :]
                        rt = rtq.next()
                        r4a = rt[:, 0:128].rearrange("p (h a j) -> p h a j", h=4, a=2)
                        r4b = rt[:, 128:256].rearrange("p (h a j) -> p h a j", h=4, a=2)
                        pr4 = p3[:, :, 64:96].rearrange("p h (a j) -> p h a j", a=2)
                        cb = cos_own[:, blk * 16:blk * 16 + 16].unsqueeze(1).unsqueeze(1).broadcast_to([128, 4, 2, 16])
                        sb_ = sin_own[:, blk * 16:blk * 16 + 16].unsqueeze(1).unsqueeze(1).broadcast_to([128, 4, 2, 16])
                        op("dve", lambda: nc.vector.tensor_copy(out=qh[:, :, 0:64], in_=p3[:, :, 0:64]), R=[pq], W=[qb_])
                        op("dve", lambda: nc.vector.tensor_tensor(out=r4a, in0=pr4, in1=cb, op=ALU.mult),
                           R=[pq, cos_own], W=[rt])
                        op("dve", lambda: nc.vector.tensor_tensor(out=r4b, in0=pr4, in1=sb_, op=ALU.mult),
                           R=[pq, sin_own], W=[rt])
                        op("dve", lambda: nc.vector.tensor_tensor(out=qh[:, :, 64:80], in0=r4a[:, :, 0, :],
                                                                  in1=r4b[:, :, 1, :], op=ALU.subtract),
                           R=[rt], W=[qb_])
                        op("dve", lambda: nc.vector.tensor_tensor(out=qh[:, :, 80:96], in0=r4a[:, :, 1, :],
                                                                  in1=r4b[:, :, 0, :], op=ALU.add),
                           R=[rt], W=[qb_])
                    qbs[j] = qb_

                def o3(j):
                    qb_ = qbs.pop(j)
                    for h in range(8):
                        op("pe", lambda: nc.tensor.transpose(out=pBt[0:96, h * 128:(h + 1) * 128],
                                                             in_=qb_[:, h * 96:(h + 1) * 96], identity=ident[:]),
                           R=[qb_, ident], W=[pBt], acc=(h > 0), signal=(h == 7))
                    qs = qbst.next()
                    evac(qs[0:96, :], pBt[0:96, :], [pBt], [qs])
                    dma("sp", QB[:, :, t0 + j * 128:t0 + j * 128 + 128].rearrange("h r t -> r h t"),
                        qs[0:96, :].rearrange("p (h t) -> p h t", h=8), R=[qs], ch=qs)

                for r_ in range(3):
                    dma("sp", QA[:, 65 + r_, t0:t0 + 512], eights[:], R=[eights], ch=eights)
                sched = {0: [lambda: o1(0)], 1: [lambda: o1(1)], 2: [lambda: o2(0)], 3: [hook1, lambda: o1(2)],
                         4: [lambda: o2(1)], 5: [lambda: o3(0)], 6: [lambda: o1(3)], 7: [lambda: o2(2)],
                         8: [lambda: o3(1)], 9: [lambda: o2(3)], 10: [lambda: o3(2)], 11: [hA[0]],
                         12: [lambda: o3(3)], 13: [hA[1]], 14: [hA[2]], 15: [hA[3]], 17: [hA[4]]}
                for s in range(20):
                    if s < 16:
                        gate(s)
                    else:
                        qa(s - 16)
                    for f in sched.get(s, []):
                        f()


            xnB = {0: prepB(xsA.pop(0))}
            if NJ > 1:
                xsA[1] = prepA(xL.pop(1))
            if NJ > 2:
                xL[2] = do_prepL(jobs[2])
            for n, job in enumerate(jobs):
                def hook1(n=n):
                    if n + 1 < NJ:
                        xnB[n + 1] = prepB(xsA.pop(n + 1))

                st3s = {}

                def mk_sq(j, n=n):
                    def f():
                        if n + 2 < NJ:
                            if j == 0:
                                st3s[n] = (ss4r.next(), ln4r.next(), rs4r.next())
                            prepA_sq(xL[n + 2], j, st3s[n])
                    return f

                def fin(n=n):
                    if n + 2 < NJ:
                        xsA[n + 2] = prepA_fin(xL.pop(n + 2), st3s.pop(n))
                    if n + 3 < NJ:
                        xL[n + 3] = do_prepL(jobs[n + 3])

                hA = [mk_sq(0), mk_sq(1), mk_sq(2), mk_sq(3), fin]

                cur = xnB.pop(n)
                evac_pat[0] = (1, 1, 1, 0) if job[0] == "all" else (0, 0, 1)
                if job[0] == "all":
                    last_all = (n + 1 >= NJ) or (jobs[n + 1][0] != "all")
                    proj_all(cur, job[1], hook1, hA, flush_kv=(True if last_all else None))
                else:
                    proj_own(cur, job[1], hook1, hA)

            qaug = sbt(p1, "qaug", [8, SO], BF16)
            tmpb = sbt(p1, "tmpb", [8, 8 * 128], BF16)
            l4 = LH[:].rearrange("h (g b t) -> h g b t", g=8, b=8)
            qa4 = qaug[:].rearrange("h (g p t) -> h g p t", g=8, p=4)
            tb3 = tmpb[:].rearrange("h (g t) -> h g t", g=8)
            for p in range(4):
                op("dve", lambda: nc.vector.tensor_scalar(out=tb3, in0=l4[:, :, 2 * p, :], scalar1=sel[:, p:p + 1],
                                                          scalar2=None, op0=ALU.mult), R=[LH, sel], W=[tmpb])
                op("dve", lambda: nc.vector.scalar_tensor_tensor(out=qa4[:, :, p, :], in0=l4[:, :, 2 * p + 1, :],
                                                                 scalar=sel[:, 4 + p:5 + p], in1=tb3,
                                                                 op0=ALU.mult, op1=ALU.add),
                   R=[LH, sel, tmpb], W=[qaug])
            dma("sp", QA[:, 64, :], qaug[:], R=[qaug], ch=qaug)
            sc.barrier()

        if stop_after <= 1:
            return nc
        Wg = sbt(top, "Wg", [128, 8 * FF], BF16, side="right")
        w3a = ExitStack()
        Wa = sbt(w3a, "Wa", [128, 4 * D], BF16)
        Wb = sbt(w3a, "Wb", [128, 4 * D], BF16)
        Wo = sbt(w3a, "Wo", [128, 8 * D], BF16)
        with ExitStack() as p2:
            mask_a = sbt(p2, "mask_a", [128, 8 * 512], BF16)
            mask_b = sbt(p2, "mask_b", [128, 8 * 512], BF16)
            KT = [sbt(p2, "KT%d" % i, [128, S], BF16) for i in range(2)]
            VV = [sbt(p2, "VV%d" % i, [128, 64 * 128], BF16) for i in range(2)]
            for i in range(2):
                v3 = VV[i][:].rearrange("p (k c) -> p k c", c=128)
                op("pool", lambda: nc.gpsimd.memset(v3[:, :, 64:128], 1.0), W=[VV[i]])
            QT = Ring(p2, "QT", 4, [128, 512], BF16)
            wg_prefetch = [False]
            PT = Ring(p2, "PT", 4, [128, 1024], BF16)
            rden = Ring(p2, "rden", 2, [64, 512], F32)
            yst = Ring(p2, "yst", 2, [64, 512], BF16)
            pS = Ring(p2, "pS", 3, [128, 1024], F32, psum=True)
            pO = Ring(p2, "pO", 2, [128, 512], F32, psum=True)

            heads = [(1, h) for h in range(NH)] + [(0, h) for h in range(NH)]
            NI = 8
            if stop_after == 2.5:
                heads = heads[:1] + heads[8:9]
            tiles = [(n, i) for n in range(len(heads)) for i in range(NI)]
            steps = [(ti, kb) for ti, (n, i) in enumerate(tiles) for kb in range(0, 8 * i + 8, 2)]

            def load_head(n):
                m, h = heads[n]
                kt, vv = KT[n % 2], VV[n % 2]
                v3 = vv[:].rearrange("p (k c) -> p k c", c=128)
                if m == 1:
                    dma("sp", kt[0:64, :], KBN[h], W=[kt], ch=kt)
                    dma("sp", kt[64:96, :], KR, W=[kt], ch=kt, group=True)
                    vsrc = VB
                else:
                    dma("sp", kt[0:64, :], KA[h, 0:64, :], W=[kt], ch=kt)
                    dma("sp", kt[64:68, :], KA[h, 64:68, :], W=[kt], ch=kt, group=True)
                    vsrc = VA
                vs3 = vsrc[:, h * 64:(h + 1) * 64].rearrange("(k p) d -> p k d", p=128)
                for q4 in range(4):
                    dma("sp", v3[:, q4 * 16:(q4 + 1) * 16, 0:64], vs3[:, q4 * 16:(q4 + 1) * 16, :], W=[vv], ch=vv,
                        group=(q4 > 0))

            qt_of = {}

            def load_q(ti):
                n, i = tiles[ti]
                m, h = heads[n]
                qt = QT.next()
                qt_of[ti] = qt
                if m == 1:
                    dma("sp", qt[0:96, :], QB[h, :, i * 512:(i + 1) * 512], W=[qt], ch=qt)
                else:
                    dma("sp", qt[0:68, :], QA[h, :, i * 512:(i + 1) * 512], W=[qt], ch=qt)

            st_S = {}
            st_P = {}
            po_of = {}

            def emit_qk(idx):
                ti, kb0 = steps[idx]
                n, i = tiles[ti]
                m, h = heads[n]
                kt = KT[n % 2]
                qt = qt_of[ti]
                kc = 96 if m == 1 else 68
                d0 = kb0 - 8 * i
                n0 = 0 if d0 < 0 else 128 * (d0 // 2)
                ps = pS.next()
                for e in range(2):
                    kb = kb0 + e
                    d = kb - 8 * i
                    pse = ps[:, e * 512 + n0:e * 512 + 512]
                    if d < 0:
                        op("pe", lambda: nc.tensor.matmul(pse, lhsT=kt[0:kc, kb * 128:(kb + 1) * 128],
                                                          rhs=qt[0:kc, n0:512], start=True, stop=True),
                           R=[kt, qt], W=[ps], acc=(e == 1), signal=(e == 1))
                    else:
                        mk = mask_b if m == 1 else mask_a
                        op("pe", lambda: nc.tensor.matmul(pse, lhsT=kt[0:kc, kb * 128:(kb + 1) * 128],
                                                          rhs=qt[0:kc, n0:512], start=True, stop=False),
                           R=[kt, qt], W=[ps], acc=(e == 1), signal=False)
                        op("pe", lambda: nc.tensor.matmul(ps[:, e * 512 + n0:e * 512 + n0 + 128], lhsT=ident[:, :],
                                                          rhs=mk[:, d * 512 + n0:d * 512 + n0 + 128], start=False, stop=True),
                           R=[ident, mk], W=[ps], acc=True, signal=(e == 1))
                pt = PT.next()
                ps3 = ps[:].rearrange("p (e q) -> p e q", e=2)[:, :, n0:512]
                pt3 = pt[:].rearrange("p (e q) -> p e q", e=2)[:, :, n0:512]
                op("act", lambda: nc.scalar.activation(out=pt3, in_=ps3, func=AF.Exp,
                                                       scale=(SC_B if m == 1 else SC_A)), R=[ps], W=[pt])
                st_P[idx] = (pt, n0)

            def emit_pv(idx):
                ti, kb0 = steps[idx]
                n, i = tiles[ti]
                m, h = heads[n]
                nkb = 8 * i + 8
                vv = VV[n % 2]
                pt, n0 = st_P.pop(idx)
                if kb0 == 0:
                    po_of[ti] = pO.next()
                    if ti + 3 < len(tiles):
                        load_q(ti + 3)
                    if i == 0 and n + 1 < len(heads):
                        load_head(n + 1)
                    if i == 1 and n == 2:
                        load_weight_cast(Wg, lambda c: Wg[:, c * FF:(c + 1) * FF],
                                         lambda c: w_g[c * 128:(c + 1) * 128, :], 8)
                    if i == 1 and not wg_prefetch[0]:
                        wg_prefetch[0] = True
                        load_weight_cast(Wa, lambda c: Wa[:, c * D:(c + 1) * D], lambda c: w_a[c * 128:(c + 1) * 128, :], 4)
                        load_weight_cast(Wb, lambda c: Wb[:, c * D:(c + 1) * D], lambda c: w_b[c * 128:(c + 1) * 128, :], 4)
                        load_weight_cast(Wo, lambda c: Wo[:, c * D:(c + 1) * D], lambda c: w_o[c * 128:(c + 1) * 128, :], 8)
                po = po_of[ti]
                for e in range(2):
                    kb = kb0 + e
                    last = (kb == nkb - 1)
                    op("pe", lambda: nc.tensor.matmul(po[:, n0:512], lhsT=vv[:, kb * 128:(kb + 1) * 128],
                                                      rhs=pt[:, e * 512 + n0:e * 512 + 512],
                                                      start=(kb == 0), stop=last),
                       R=[vv, pt], W=[po], acc=(kb > 0), signal=(e == 1))
                if last:
                    rd = rden.next()
                    ys = yst.next()
                    op("dve", lambda: nc.vector.reciprocal(out=rd[0:64, :], in_=po[64:128, :]), R=[po], W=[rd])
                    op("dve", lambda: nc.vector.tensor_tensor(out=ys[0:64, :], in0=po[0:64, :], in1=rd[0:64, :],
                                                              op=ALU.mult), R=[po, rd], W=[ys])
                    r0 = (0 if m == 0 else 512) + h * 64
                    for uu in range(2):
                        dma("pool", YT[2 * i + uu, (r0 % 128):(r0 % 128) + 64, r0 // 128, :],
                            ys[0:64, uu * 256:(uu + 1) * 256], R=[ys], ch=ys)
                    del po_of[ti]

            load_head(0)
            dma("pool", mask_b[:], mask_b_in, W=[mask_b], ch=mask_b)
            dma("pool", mask_a[:], mask_a_in, W=[mask_a], ch=mask_a)
            load_q(0)
            load_q(1)
            load_q(2)
            LA = 2
            for idx in range(len(steps) + LA):
                if idx < len(steps):
                    emit_qk(idx)
                if idx - LA >= 0:
                    emit_pv(idx - LA)
            sc.barrier()

        if stop_after <= 2.5:
            return nc

        Wu = sbt(top, "Wu", [128, 8 * FF], BF16, side="right")
        with ExitStack() as p3:
            gffn_b = sbt(p3, "gffn_b", [128, D], F32)
            dma("sp", gffn_b[:], g_ffn_row.partition_broadcast(128), W=[gffn_b], ch=gffn_b)
            T3 = 256
            yT = Ring(p3, "yT", 2, [128, 8 * T3], BF16)
            gT = Ring(p3, "gT", 2, [128, 16 * T3], BF16)
            xblk = Ring(p3, "xb3", 5, [128, D], F32)
            t1r = Ring(p3, "t1r", 2, [128, T3], F32)
            t2r = Ring(p3, "t2r", 2, [128, T3], F32)
            mT = Ring(p3, "mT", 2, [128, 8 * T3], BF16)
            sqr = Ring(p3, "sq3", 2, [128, D], BF16)
            hsr = Ring(p3, "hs3", 4, [128, D], BF16)
            ssr = Ring(p3, "ss3", 4, [128, 1], F32)
            lnr = Ring(p3, "ln3", 4, [128, 1], F32)
            rsr = Ring(p3, "rs3", 4, [128, 1], F32)
            hnT = Ring(p3, "hnT3", 2, [128, 8 * T3], BF16)
            pa_r = Ring(p3, "pa", 2, [128, 512], F32, psum=True)
            pb_r = Ring(p3, "pb", 2, [128, 512], F32, psum=True)
            po_r = Ring(p3, "po3", 2, [128, 512], F32, psum=True)
            ptx = Ring(p3, "ptx3", 2, [128, 1024], BF16, psum=True)
            NB3 = T3 // 128

            def load_tile3(i):
                y = yT.next()
                g = gT.next()
                dma("sp", y[:], YT[i].rearrange("p c t -> p (c t)"), W=[y], ch=y)
                dma("sp", g[:], GT[i].rearrange("p c t -> p (c t)"), W=[g], ch=g)
                return y, g

            pend = []

            def flush_one():
                hs, hn, j, ti = pend.pop(0)
                pt = ptx.next()
                for kc in range(8):
                    op("pe", lambda: nc.tensor.transpose(out=pt[:, kc * 128:(kc + 1) * 128],
                                                         in_=hs[:, kc * 128:(kc + 1) * 128], identity=ident[:]),
                       R=[hs, ident], W=[pt], acc=(kc > 0), signal=(kc == 7))
                op("act", lambda: nc.scalar.copy(
                    out=hn[:].rearrange("p (c t) -> p c t", c=8)[:, :, j * 128:(j + 1) * 128],
                    in_=pt[:].rearrange("p (c t) -> p c t", c=8)), R=[pt], W=[hn])
                if j == NB3 - 1:
                    dma("sp", HNT[ti], hn[:], R=[hn], ch=hn)

            NT3 = SO // T3

            def branch(i, y, g):
                m_ = mT.next()
                for mc in range(8):
                    pa, pb = pa_r.next(), pb_r.next()
                    for kc in range(4):
                        op("pe", lambda: nc.tensor.matmul(pa[:, 0:T3], lhsT=Wa[:, kc * D + mc * 128:kc * D + mc * 128 + 128],
                                                          rhs=y[:, kc * T3:(kc + 1) * T3], start=(kc == 0), stop=(kc == 3)),
                           R=[Wa, y], W=[pa], acc=(kc > 0), signal=(kc == 3))
                    for kc in range(4):
                        op("pe", lambda: nc.tensor.matmul(pb[:, 0:T3], lhsT=Wb[:, kc * D + mc * 128:kc * D + mc * 128 + 128],
                                                          rhs=y[:, (4 + kc) * T3:(5 + kc) * T3], start=(kc == 0),
                                                          stop=(kc == 3)),
                           R=[Wb, y], W=[pb], acc=(kc > 0), signal=(kc == 3))
                    t1, t2 = t1r.next(), t2r.next()
                    op("dve", lambda: nc.vector.tensor_tensor(out=t1[:], in0=pa[:, 0:T3], in1=g[:, mc * T3:(mc + 1) * T3],
                                                              op=ALU.mult), R=[pa, g], W=[t1])
                    op("dve", lambda: nc.vector.tensor_tensor(out=t2[:], in0=pb[:, 0:T3],
                                                              in1=g[:, (8 + mc) * T3:(9 + mc) * T3], op=ALU.mult),
                       R=[pb, g], W=[t2])
                    op("pool", lambda: nc.gpsimd.tensor_tensor(out=m_[:, mc * T3:(mc + 1) * T3], in0=t1[:], in1=t2[:],
                                                               op=ALU.add), R=[t1, t2], W=[m_])
                    if mc in (1, 5) and pend:
                        flush_one()
                return m_

            def load_x3(i):
                xbs = []
                for j in range(NB3):
                    t0 = i * T3 + j * 128
                    xb = xblk.next()
                    dma("sp", xb[:], x_own[t0:t0 + 128, :], W=[xb], ch=xb)
                    xbs.append(xb)
                return xbs

            def outproj(i, m_, xbs):
                hn = hnT.next()
                for j in range(NB3):
                    t0 = i * T3 + j * 128
                    xb = xbs[j]
                    for hf in range(2):
                        po = po_r.next()
                        for kc in range(8):
                            op("pe", lambda: nc.tensor.matmul(po[:, :], lhsT=m_[:, kc * T3 + j * 128:kc * T3 + j * 128 + 128],
                                                              rhs=Wo[:, kc * D + hf * 512:kc * D + hf * 512 + 512],
                                                              start=(kc == 0), stop=(kc == 7)),
                               R=[Wo, m_], W=[po], acc=(kc > 0), signal=(kc == 7))
                        op("dve", lambda: nc.vector.tensor_tensor(out=xb[:, hf * 512:(hf + 1) * 512], in0=po[:],
                                                                  in1=xb[:, hf * 512:(hf + 1) * 512], op=ALU.add),
                           R=[po, xb], W=[xb])
                    dma("sp", HH[t0:t0 + 128, :], xb[:], R=[xb], ch=xb)
                    sq = sqr.next()
                    op("act", lambda: nc.scalar.activation(out=sq[:], in_=xb[:], func=AF.Square), R=[xb], W=[sq])
                    ss, ln_, rs = ssr.next(), lnr.next(), rsr.next()
                    op("dve", lambda: nc.vector.tensor_reduce(out=ss[:], in_=sq[:], axis=AX.X, op=ALU.add),
                       R=[sq], W=[ss])
                    emit_rstd(ss, ln_, rs, 1, 1.0 / D)
                    hs = hsr.next()
                    op("dve", lambda: nc.vector.scalar_tensor_tensor(out=hs[:], in0=xb[:], scalar=rs[:, 0:1],
                                                                     in1=gffn_b[:], op0=ALU.mult, op1=ALU.mult),
                       R=[xb, rs, gffn_b], W=[hs])
                    pend.append((hs, hn, j, i))

            tl = {0: load_tile3(0)}
            xl = {0: load_x3(0)}
            if NT3 > 1:
                tl[1] = load_tile3(1)
            ms = {0: branch(0, *tl[0])}
            for i in range(NT3):
                if i % 2 == 0:
                    c_ = i // 2
                    dma("pool", Wu[:, c_ * FF:(c_ + 1) * FF], w_u[c_ * 128:(c_ + 1) * 128, :], W=[Wu], ch=Wu,
                        group=(c_ > 0), max_dma_last_dim=2048)
                if i + 1 < NT3:
                    xl[i + 1] = load_x3(i + 1)
                    ms[i + 1] = branch(i + 1, *tl.pop(i + 1))
                    tl.pop(i, None)
                    if i + 2 < NT3:
                        tl[i + 2] = load_tile3(i + 2)
                outproj(i, ms.pop(i), xl.pop(i))
            while pend:
                flush_one()
            sc.barrier()
        w3a.close()

        if stop_after <= 3:
            return nc

        with ExitStack() as p4:
            Wd = sbt(p4, "Wd", [128, NFC * D], BF16)
            gfin = sbt(p4, "gfin", [128, D], F32)
            dma("sp", gfin[:], g_fin.partition_broadcast(128), W=[gfin], ch=gfin)
            TW = 256
            hnr = Ring(p4, "hn4", 2, [128, 8 * TW], BF16)
            hbr = Ring(p4, "hb4", 4, [128, D], F32)
            sgr = Ring(p4, "sg4", 2, [128, TW], F32)
            actT = Ring(p4, "actT", 2, [128, NFC * TW], BF16)
            sqr = Ring(p4, "sq4", 2, [128, D], BF16)
            yor = Ring(p4, "yo4", 2, [128, D], F32)
            ssr = Ring(p4, "ss4", 4, [128, 1], F32)
            lnr = Ring(p4, "ln4", 4, [128, 1], F32)
            rsr = Ring(p4, "rs4", 4, [128, 1], F32)
            pg_r = Ring(p4, "pg", 2, [128, 512], F32, psum=True)
            pu_r = Ring(p4, "pu", 2, [128, 512], F32, psum=True)
            po_r = Ring(p4, "po4", 4, [128, 512], F32, psum=True)
            NT4 = SO // TW

            def load_tile4(u):
                hn = hnr.next()
                dma("sp", hn[:], HNT[u], W=[hn], ch=hn)
                hbs = []
                for j in range(TW // 128):
                    hb = hbr.next()
                    t0 = u * TW + j * 128
                    dma("sp", hb[:], HH[t0:t0 + 128, :], W=[hb], ch=hb)
                    hbs.append(hb)
                return hn, hbs

            nxt4 = load_tile4(0)
            load_weight_cast(Wd, lambda c: Wd[:, c * D:(c + 1) * D], lambda c: w_d[c * 128:(c + 1) * 128, :], NFC)
            for u in range(NT4):
                hn, hbs = nxt4
                if u + 1 < NT4:
                    nxt4 = load_tile4(u + 1)
                at = actT.next()
                for f in range(NFC):
                    pg, pu = pg_r.next(), pu_r.next()
                    for kc in range(8):
                        op("pe", lambda: nc.tensor.matmul(pg[:, 0:TW], lhsT=Wg[:, kc * FF + f * 128:kc * FF + f * 128 + 128],
                                                          rhs=hn[:, kc * TW:(kc + 1) * TW], start=(kc == 0), stop=(kc == 7)),
                           R=[Wg, hn], W=[pg], acc=(kc > 0), signal=(kc == 7))
                    for kc in range(8):
                        op("pe", lambda: nc.tensor.matmul(pu[:, 0:TW], lhsT=Wu[:, kc * FF + f * 128:kc * FF + f * 128 + 128],
                                                          rhs=hn[:, kc * TW:(kc + 1) * TW], start=(kc == 0), stop=(kc == 7)),
                           R=[Wu, hn], W=[pu], acc=(kc > 0), signal=(kc == 7))
                    sg = sgr.next()
                    op("act", lambda: nc.scalar.activation(out=sg[:], in_=pg[:, 0:TW], func=AF.Silu), R=[pg], W=[sg])
                    op("dve", lambda: nc.vector.tensor_tensor(out=at[:, f * TW:(f + 1) * TW], in0=pu[:, 0:TW], in1=sg[:],
                                                              op=ALU.mult), R=[pu, sg], W=[at])
                for j in range(TW // 128):
                    hb = hbs[j]
                    t0 = u * TW + j * 128
                    for hf in range(2):
                        po = po_r.next()
                        for f in range(NFC):
                            op("pe", lambda: nc.tensor.matmul(po[:, :], lhsT=at[:, f * TW + j * 128:f * TW + j * 128 + 128],
                                                              rhs=Wd[:, f * D + hf * 512:f * D + hf * 512 + 512],
                                                              start=(f == 0), stop=(f == NFC - 1)),
                               R=[Wd, at], W=[po], acc=(f > 0), signal=(f == NFC - 1))
                        op("dve", lambda: nc.vector.tensor_tensor(out=hb[:, hf * 512:(hf + 1) * 512], in0=po[:],
                                                                  in1=hb[:, hf * 512:(hf + 1) * 512], op=ALU.add),
                           R=[po, hb], W=[hb])
                    sq = sqr.next()
                    op("pool", lambda: nc.gpsimd.tensor_tensor(out=sq[:], in0=hb[:], in1=hb[:], op=ALU.mult),
                       R=[hb], W=[sq])
                    ss, ln_, rs = ssr.next(), lnr.next(), rsr.next()
                    op("dve", lambda: nc.vector.tensor_reduce(out=ss[:], in_=sq[:], axis=AX.X, op=ALU.add),
                       R=[sq], W=[ss])
                    emit_rstd(ss, ln_, rs, 1, 1.0 / D)
                    yo = yor.next()
                    op("dve", lambda: nc.vector.scalar_tensor_tensor(out=yo[:], in0=hb[:], scalar=rs[:, 0:1], in1=gfin[:],
                                                                     op0=ALU.mult, op1=ALU.mult),
                       R=[hb, rs, gfin], W=[yo])
                    dma("sp", out_own[t0:t0 + 128, :], yo[:], R=[yo], ch=yo)
            sc.barrier()
    return nc


def own_token_index(hh):
    idx = []
    for g in range(8):
        for p in range(4):
            b0 = (g * 8 + OWN[hh][p]) * 128
            idx.append(np.arange(b0, b0 + 128))
    return np.concatenate(idx)


def make_masks(hh):
    ma = np.zeros((128, 8, 512), np.float32)
    mb = np.zeros((128, 8, 512), np.float32)
    k = np.arange(128)[:, None]
    for kb in range(8):
        kpos = kb * 128 + k
        for p in range(4):
            qpos = OWN[hh][p] * 128 + np.arange(128)[None, :]
            ma[:, kb, p * 128:(p + 1) * 128] = np.where(kpos <= qpos, 0.0, NEG)
            mb[:, kb, p * 128:(p + 1) * 128] = np.where(kpos // 64 <= qpos // 64, 0.0, NEG)
    return ma.reshape(128, 4096), mb.reshape(128, 4096)


def prep_core_inputs(c, inp, shared):
    b, hh = c // 2, c % 2
    own = own_token_index(hh)
    xb = np.asarray(inp["x"][b], dtype=np.float32)
    pos = np.asarray(inp["positions"][b]).astype(np.int32)
    ma, mb = make_masks(hh)
    s = np.array([1.0 if OWN[hh][p] == 2 * p else 0.0 for p in range(4)], np.float32)
    sel = np.tile(np.concatenate([-s, -(1.0 - s)])[None, :], (8, 1)).astype(np.float32)
    d = dict(shared)
    d.update({
        "x_all": np.ascontiguousarray(xb),
        "x_own": np.ascontiguousarray(xb[own]),
        "pos_all": np.ascontiguousarray(pos.reshape(64, 128).T),
        "pos_own": np.ascontiguousarray(pos[own].reshape(32, 128).T),
        "mask_a": ma, "mask_b": mb, "sel": sel,
    })
    return d


def prep_shared(inp):
    f = lambda a: np.ascontiguousarray(np.asarray(a, dtype=np.float32))
    wkv = f(inp["w_kv_up"][0]).reshape(128, 8, 2, 64)
    wkv = np.concatenate([wkv[:, :, 0, :].reshape(128, 512), wkv[:, :, 1, :].reshape(128, 512)], axis=1)
    half = 16
    invf = (np.float32(10000.0) ** (-np.arange(half, dtype=np.float32) / np.float32(half))).astype(np.float32)
    return {
        "w_in": f(inp["w_in"][0]),
        "w_q": f(inp["w_q_up"][0]),
        "w_kv": np.ascontiguousarray(wkv),
        "w_a": f(inp["w_branch_a"][0]),
        "w_b": f(inp["w_branch_b"][0]),
        "w_o": f(inp["w_out"][0]),
        "w_g": f(inp["w_ffn_gate"][0]),
        "w_u": f(inp["w_ffn_up"][0]),
        "w_d": f(inp["w_ffn_down"][0]),
        "g_mix": np.ascontiguousarray(f(inp["norm_mix_g"][0]).reshape(8, 128).T),
        "g_ffn": np.ascontiguousarray(f(inp["norm_ffn_g"][0]).reshape(8, 128).T),
        "g_q": np.ascontiguousarray(f(inp["q_a_norm_g"][0]).reshape(2, 128).T),
        "g_kv": np.ascontiguousarray(f(inp["kv_a_norm_g"][0]).reshape(1, 128).T),
        "g_fin": f(inp["norm_final_g"]).reshape(1, D),
        "g_ffn_row": f(inp["norm_ffn_g"][0]).reshape(1, D),
        "b_f": f(inp["b_forget"][0]).reshape(8, 1),
        "b_gate": np.ascontiguousarray(f(inp["b_gate"][0]).reshape(16, 128).T),
        "ident": np.eye(128, dtype=np.float32),
        "invf": np.tile(invf[None, :], (128, 1)).astype(np.float32),
    }


_NC_CACHE = {}


def kernel(**inputs):
    if "nc" not in _NC_CACHE:
        _NC_CACHE["nc"] = build_program()
    nc = _NC_CACHE["nc"]
    shared = prep_shared(inputs)
    in_maps = [prep_core_inputs(c, inputs, shared) for c in range(8)]
    res = run_bass_kernel_spmd(nc, in_maps, core_ids=list(range(8)))
    out = np.empty((4, S, D), np.float32)
    for c in range(8):
        b, hh = c // 2, c % 2
        out[b, own_token_index(hh)] = np.asarray(res.results[c]["out_own"], dtype=np.float32)
    return out
```

```python
from contextlib import ExitStack
import numpy as np
import ml_dtypes
FEAT = set('ka,fa,va,bs,bs2,bs3,bs4,kvup,own,qa,gate,cq,scan'.split(','))
import concourse.bass as bass
import concourse.mybir as mybir
from concourse.bass_utils import run_bass_kernel_spmd

F32 = mybir.dt.float32
BF16 = mybir.dt.bfloat16
I32 = mybir.dt.int32
AF = mybir.ActivationFunctionType
ALU = mybir.AluOpType
AX = mybir.AxisListType

S = 8192
SO = 4096
D = 1024
NH = 8
FF = 2816
NFC = FF // 128
EPS = 1e-6
OWN = ([0, 3, 5, 6], [1, 2, 4, 7])
NEG = -30000.0
O_QA, O_KA, O_VA, O_FA, O_CQ, O_CKV, O_KR, O_GA, O_GB = 0, 512, 1024, 1536, 1544, 1800, 1928, 1960, 2984
INW = 4008
SC_A = 64 ** -0.5
SC_B = 96 ** -0.5
TWO_PI = 2.0 * np.pi
CW1 = 6.28125
CW2 = float(np.float32(TWO_PI - 6.28125))


class Buf:
    __slots__ = ("name", "t", "w", "r", "ch", "chv", "excl")

    def __init__(self, name, t=None, excl=False):
        self.name = name
        self.excl = excl
        self.t = t
        self.w = None
        self.r = {}
        self.ch = None
        self.chv = 0

    def __getitem__(self, k):
        return self.t[k]


class Sched:
    def __init__(self, nc):
        self.nc = nc
        self.eng = dict(pe=nc.tensor, act=nc.scalar, dve=nc.vector, pool=nc.gpsimd, sp=nc.sync)
        self.sems = {}
        self.cnt = {}
        for k in ("pe", "act", "dve", "pool"):
            self.sems["sem_" + k] = nc.alloc_semaphore("sem_" + k)
            self.cnt[k] = 0
        self.known = {k: {} for k in self.eng}
        self.chans = []
        self.pe_pending = False
        self.nwait = 0

    def _deps(self, e, R, W, acc):
        deps = []
        own = "sem_" + e
        for b in R:
            if b.w is not None:
                deps.append(b.w)
            if b.excl:
                deps.extend((k, v) for k, v in b.r.items() if k != own)
        if not acc:
            for b in W:
                if b.w is not None:
                    deps.append(b.w)
                deps.extend(b.r.items())
        kn = self.known[e]
        for sn, val in deps:
            if e == "pe" and sn == "sem_pe":
                continue
            if kn.get(sn, 0) >= val:
                continue
            self.eng[e].wait_ge(self.sems[sn], val)
            self.nwait += 1
            kn[sn] = val

    def _mark(self, ev, R, W):
        for b in R:
            if b.r.get(ev[0], 0) < ev[1]:
                b.r[ev[0]] = ev[1]
        for b in W:
            b.w = ev
            b.r = {}

    def op(self, e, fn, R=(), W=(), acc=False, signal=True):
        self._deps(e, R, W, acc)
        inst = fn()
        sn = "sem_" + e
        if signal:
            self.cnt[e] += 1
            inst.then_inc(self.sems[sn], 1)
            ev = (sn, self.cnt[e])
            if e == "pe":
                self.pe_pending = False
        else:
            assert e == "pe"
            ev = (sn, self.cnt[e] + 1)
            self.pe_pending = True
        self._mark(ev, R, W)

    def dma(self, q, out, in_, R=(), W=(), ch=None, group=False, **kw):
        self._deps(q, R, W, group)
        if ch.ch is None:
            ch.ch = "ch%d" % len(self.chans)
            self.sems[ch.ch] = self.nc.alloc_semaphore(ch.ch)
            self.chans.append(ch)
        inst = self.eng[q].dma_start(out=out, in_=in_, **kw)
        ch.chv += 16
        inst.then_inc(self.sems[ch.ch], 16)
        self._mark((ch.ch, ch.chv), R, W)

    def barrier(self):
        assert not self.pe_pending
        evs = [("sem_" + k, v) for k, v in self.cnt.items() if v > 0]
        evs += [(b.ch, b.chv) for b in self.chans if b.chv > 0]
        for e in self.eng:
            kn = self.known[e]
            for sn, val in evs:
                if kn.get(sn, 0) >= val:
                    continue
                self.eng[e].wait_ge(self.sems[sn], val)
                kn[sn] = val


def build_program(debug=False, stop_after=9):
    nc = bass.Bass("TRN2", target_bir_lowering=False)
    sc = Sched(nc)
    op = sc.op
    dma = sc.dma

    def din(name, shape, dt=F32):
        return nc.dram_tensor(name, list(shape), dt, kind="ExternalInput").ap()

    def dscr(name, shape, dt):
        return nc.dram_tensor(name, list(shape), dt, kind="ExternalOutput" if debug else "Internal").ap()

    x_all = din("x_all", [S, D])
    x_own = din("x_own", [SO, D])
    pos_all = din("pos_all", [128, 64], I32)
    pos_own = din("pos_own", [128, 32], I32)
    w_in = din("w_in", [D, INW])
    w_q = din("w_q", [256, 768])
    w_kv = din("w_kv", [128, 1024])
    w_a = din("w_a", [512, D])
    w_b = din("w_b", [512, D])
    w_o = din("w_o", [D, D])
    w_g = din("w_g", [D, FF])
    w_u = din("w_u", [D, FF])
    w_d = din("w_d", [FF, D])
    g_mix = din("g_mix", [128, 8])
    g_ffn = din("g_ffn", [128, 8])
    g_q = din("g_q", [128, 2])
    g_kv = din("g_kv", [128, 1])
    g_fin = din("g_fin", [1, D])
    g_ffn_row = din("g_ffn_row", [1, D])
    b_f = din("b_f", [8, 1])
    b_gate = din("b_gate", [128, 16])
    ident_in = din("ident", [128, 128])
    invf_in = din("invf", [128, 16])
    mask_a_in = din("mask_a", [128, 8 * 512])
    mask_b_in = din("mask_b", [128, 8 * 512])
    sel_in = din("sel", [8, 8])
    out_own = nc.dram_tensor("out_own", [SO, D], F32, kind="ExternalOutput").ap()

    KA = dscr("scr_ka", [NH, 68, S], BF16)
    KBN = dscr("scr_kbn", [NH, 64, S], BF16)
    KR = dscr("scr_kr", [32, S], BF16)
    VA = dscr("scr_va", [S, 512], BF16)
    VB = dscr("scr_vb", [S, 512], BF16)
    QA = dscr("scr_qa", [NH, 68, SO], BF16)
    QB = dscr("scr_qb", [NH, 96, SO], BF16)
    GT = dscr("scr_gt", [16, 128, 16, 256], BF16)
    YT = dscr("scr_yt", [16, 128, 8, 256], BF16)
    HH = dscr("scr_h", [SO, D], F32)
    HNT = dscr("scr_hnt", [16, 128, 8 * 256], BF16)

    with ExitStack() as top:
        def sbt(es, name, shape, dt, side=None):
            return Buf(name, es.enter_context(nc.sbuf_tensor("sb_" + name, list(shape), dt, side=side)))

        def pst(es, name, shape, dt):
            return Buf(name, es.enter_context(nc.psum_tensor("ps_" + name, list(shape), dt)), excl=True)

        ident = sbt(top, "ident", [128, 128], BF16)
        identf = sbt(top, "identf", [128, 128], F32)
        dma("sp", identf[:], ident_in, W=[identf], ch=identf)
        op("dve", lambda: nc.vector.tensor_copy(out=ident[:], in_=identf[:]), R=[identf], W=[ident])

        def emit_rstd(ss, lnv, rstd, n, inv_n, P=128):
            op("act", lambda: nc.scalar.activation(out=lnv[0:P, 0:n], in_=ss[0:P, 0:n], func=AF.Ln,
                                                    bias=epsb[0:P, 0:1], scale=inv_n),
               R=[ss, epsb], W=[lnv])
            op("act", lambda: nc.scalar.activation(out=rstd[0:P, 0:n], in_=lnv[0:P, 0:n], func=AF.Exp,
                                                    scale=-0.5),
               R=[lnv], W=[rstd])

        epsb = sbt(top, "epsb", [128, 1], F32)
        op("dve", lambda: nc.vector.memset(epsb[:], EPS), W=[epsb])

        def load_weight(es_stage, dst, dst_ap_fn, src_ap_fn, nchunks, width, gain=None, gain_col=None,
                        stage_ring=None, eng_cycle=("dve", "pool")):
            for c in range(nchunks):
                st = stage_ring[c % len(stage_ring)]
                dma("sp", st[:, 0:width], src_ap_fn(c), W=[st], ch=st)
                e = eng_cycle[c % len(eng_cycle)]
                E = nc.vector if e == "dve" else nc.gpsimd
                if gain is not None and e == "act":
                    gc = gain_col(c)
                    op("act", lambda st=st, c=c, gc=gc: nc.scalar.activation(
                        out=dst_ap_fn(c), in_=st[:, 0:width], func=AF.Copy, scale=gain[:, gc:gc + 1]),
                       R=[st, gain], W=[dst])
                elif gain is not None:
                    gc = gain_col(c)
                    op(e, lambda E=E, st=st, c=c, gc=gc: E.tensor_scalar(
                        out=dst_ap_fn(c), in0=st[:, 0:width], scalar1=gain[:, gc:gc + 1], scalar2=1.0,
                        op0=ALU.mult, op1=ALU.mult), R=[st, gain], W=[dst])
                else:
                    op(e, lambda E=E, st=st, c=c: E.tensor_copy(out=dst_ap_fn(c), in_=st[:, 0:width]),
                       R=[st], W=[dst])

        def load_weight_cast(dst, dst_ap_fn, src_ap_fn, nchunks):
            for c in range(nchunks):
                dma("pool", dst_ap_fn(c), src_ap_fn(c), W=[dst], ch=dst, group=(c > 0), max_dma_last_dim=2048)

        oneb = sbt(top, "oneb", [128, 1], F32)
        op("dve", lambda: nc.vector.memset(oneb[:], 1.0), W=[oneb])

        class Ring:
            def __init__(self, es, name, n, shape, dt, psum=False):
                mk = pst if psum else sbt
                self.b = [mk(es, "%s%d" % (name, i), shape, dt) for i in range(n)]
                self.i = 0

            def next(self):
                b = self.b[self.i % len(self.b)]
                self.i += 1
                return b

        with ExitStack() as p1:
            Win = sbt(p1, "Win", [128, 8 * INW], BF16)
            Wkv = sbt(p1, "Wkv", [128, 1024], BF16)
            Wq = sbt(p1, "Wq", [128, 2 * 768], BF16)
            gmix = sbt(p1, "gmix", [128, 8], F32)
            gq = sbt(p1, "gq", [128, 2], F32)
            gkv = sbt(p1, "gkv", [128, 1], F32)
            bg = sbt(p1, "bg", [128, 16], F32)
            negb = sbt(p1, "negb", [8, 1], F32)
            bfr = sbt(p1, "bfr", [8, 1], F32)
            sel = sbt(p1, "sel", [8, 8], F32)
            invf = sbt(p1, "invf", [128, 16], F32)
            cos_all = sbt(p1, "cos_all", [128, 64 * 16], F32)
            sin_all = sbt(p1, "sin_all", [128, 64 * 16], F32)
            cos_own = sbt(p1, "cos_own", [128, 32 * 16], F32)
            sin_own = sbt(p1, "sin_own", [128, 32 * 16], F32)
            LH = sbt(p1, "LH", [8, S], BF16)
            ones8 = sbt(p1, "ones8", [8, 512], BF16)
            op("pool", lambda: nc.gpsimd.memset(ones8[:], 1.0), W=[ones8])
            for (dst, src) in ((gmix, g_mix), (gq, g_q), (gkv, g_kv), (bg, b_gate), (bfr, b_f), (sel, sel_in),
                               (invf, invf_in)):
                dma("sp", dst[:], src, W=[dst], ch=dst)
            op("dve", lambda: nc.vector.tensor_scalar(out=negb[:], in0=bfr[:], scalar1=-1.0, scalar2=None,
                                                      op0=ALU.mult), R=[bfr], W=[negb])

            xblk = Ring(p1, "xblk", 6, [128, D], F32)
            sqr = Ring(p1, "sqr", 2, [128, D], BF16)
            xsr = Ring(p1, "xsr", 5, [128, D], BF16)
            ssr = Ring(p1, "ssr", 4, [128, 1], F32)
            lnr = Ring(p1, "lnr", 4, [128, 1], F32)
            rsr = Ring(p1, "rsr", 4, [128, 1], F32)
            ss4r = Ring(p1, "ss4r", 3, [128, 4], F32)
            ln4r = Ring(p1, "ln4r", 3, [128, 4], F32)
            rs4r = Ring(p1, "rs4r", 3, [128, 4], F32)
            def prepL(xsrc, T):
                xbs = []
                for j in range(4):
                    xb = xblk.next()
                    t0 = T * 512 + j * 128
                    dma("pool", xb[:], xsrc[t0:t0 + 128, :], W=[xb], ch=xb)
                    xbs.append(xb)
                return xbs

            def prepA_sq(xbs, j, st3):
                xb = xbs[j]
                sq = sqr.next()
                op("act", lambda: nc.scalar.activation(out=sq[:], in_=xb[:], func=AF.Square), R=[xb], W=[sq])
                op("dve", lambda: nc.vector.tensor_reduce(out=st3[0][:, j:j + 1], in_=sq[:], axis=AX.X, op=ALU.add),
                   R=[sq], W=[st3[0]])

            def prepA_fin(xbs, st3):
                ss4, ln4, rs4 = st3
                emit_rstd(ss4, ln4, rs4, 4, 1.0 / D)
                xss = []
                for j in range(4):
                    xs = xsr.next()
                    xb = xbs[j]
                    op("dve", lambda: nc.vector.tensor_scalar(out=xs[:], in0=xb[:], scalar1=rs4[:, j:j + 1],
                                                              scalar2=None, op0=ALU.mult), R=[xb, rs4], W=[xs])
                    xss.append(xs)
                return xss

            def prepA(xbs):
                st3 = (ss4r.next(), ln4r.next(), rs4r.next())
                for j in range(4):
                    prepA_sq(xbs, j, st3)
                return prepA_fin(xbs, st3)

            jobs = [("all", T) for T in range(16)] + [("own", T) for T in range(8)]
            if stop_after == 0:
                jobs = jobs[:1] + (jobs[16:17] if 'own' in FEAT else [])
            def do_prepL(job):
                return prepL(x_all if job[0] == "all" else x_own, job[1])

            NJ = len(jobs)
            xL = {0: do_prepL(jobs[0])}
            xsA = {0: prepA(xL.pop(0))}
            if NJ > 1:
                xL[1] = do_prepL(jobs[1])

            with ExitStack() as tmp:
                def rope_tables(pos_src, nb, cos_t, sin_t, tag):
                    n = nb * 16
                    pi_ = sbt(tmp, "pi_" + tag, [128, nb], I32)
                    pf = sbt(tmp, "pf_" + tag, [128, nb], F32)
                    ang = sbt(tmp, "ang_" + tag, [128, n], F32)
                    t1 = sbt(tmp, "t1_" + tag, [128, n], F32)
                    ki = sbt(tmp, "ki_" + tag, [128, n], I32)
                    dma("sp", pi_[:], pos_src, W=[pi_], ch=pi_)
                    op("dve", lambda: nc.vector.tensor_copy(out=pf[:], in_=pi_[:]), R=[pi_], W=[pf])
                    a3 = ang[:].rearrange("p (b j) -> p b j", j=16)
                    op("dve", lambda: nc.vector.tensor_tensor(
                        out=a3, in0=pf[:].unsqueeze(2).broadcast_to([128, nb, 16]),
                        in1=invf[:].unsqueeze(1).broadcast_to([128, nb, 16]), op=ALU.mult),
                       R=[pf, invf], W=[ang])
                    op("dve", lambda: nc.vector.tensor_scalar(out=t1[:], in0=ang[:], scalar1=float(1.0 / TWO_PI),
                                                              scalar2=None, op0=ALU.mult), R=[ang], W=[t1])
                    op("dve", lambda: nc.vector.tensor_copy(out=ki[:], in_=t1[:]), R=[t1], W=[ki])
                    op("dve", lambda: nc.vector.tensor_copy(out=t1[:], in_=ki[:]), R=[ki], W=[t1])
                    op("dve", lambda: nc.vector.scalar_tensor_tensor(out=ang[:], in0=t1[:], scalar=-CW1, in1=ang[:],
                                                                     op0=ALU.mult, op1=ALU.add),
                       R=[t1, ang], W=[ang])
                    op("dve", lambda: nc.vector.scalar_tensor_tensor(out=ang[:], in0=t1[:], scalar=-CW2, in1=ang[:],
                                                                     op0=ALU.mult, op1=ALU.add),
                       R=[t1, ang], W=[ang])
                    PI_ = float(np.pi)
                    op("dve", lambda: nc.vector.tensor_single_scalar(out=t1[:], in_=ang[:], scalar=PI_, op=ALU.is_gt),
                       R=[ang], W=[t1])
                    op("dve", lambda: nc.vector.scalar_tensor_tensor(out=ang[:], in0=t1[:], scalar=-float(TWO_PI),
                                                                     in1=ang[:], op0=ALU.mult, op1=ALU.add),
                       R=[t1, ang], W=[ang])
                    op("dve", lambda: nc.vector.tensor_single_scalar(out=t1[:], in_=ang[:], scalar=-PI_, op=ALU.is_lt),
                       R=[ang], W=[t1])
                    op("dve", lambda: nc.vector.scalar_tensor_tensor(out=ang[:], in0=t1[:], scalar=float(TWO_PI),
                                                                     in1=ang[:], op0=ALU.mult, op1=ALU.add),
                       R=[t1, ang], W=[ang])
                    PC = 3.1415925
                    op("dve", lambda: nc.vector.tensor_scalar(out=ang[:], in0=ang[:], scalar1=PC, scalar2=-PC,
                                                              op0=ALU.min, op1=ALU.max), R=[ang], W=[ang])
                    op("act", lambda: nc.scalar.activation(out=sin_t[:], in_=ang[:], func=AF.Sin),
                       R=[ang], W=[sin_t])
                    op("dve", lambda: nc.vector.tensor_scalar(out=t1[:], in0=ang[:], scalar1=-1.0, scalar2=None,
                                                              op0=ALU.mult), R=[ang], W=[t1])
                    op("dve", lambda: nc.vector.tensor_tensor(out=t1[:], in0=t1[:], in1=ang[:], op=ALU.max),
                       R=[ang, t1], W=[t1])
                    op("dve", lambda: nc.vector.tensor_scalar(out=t1[:], in0=t1[:], scalar1=-1.0,
                                                              scalar2=float(np.pi / 2), op0=ALU.mult, op1=ALU.add),
                       R=[t1], W=[t1])
                    op("act", lambda: nc.scalar.activation(out=cos_t[:], in_=t1[:], func=AF.Sin),
                       R=[t1], W=[cos_t])

                rope_tables(pos_all, 64, cos_all, sin_all, "a")
                rope_tables(pos_own, 32, cos_own, sin_own, "o")
                wst = [sbt(tmp, "wst%d" % i, [128, 2048], F32) for i in range(2)]
                HW = INW // 2
                load_weight(tmp, Win,
                            lambda c: Win[:, (c // 2) * INW + (c % 2) * HW:(c // 2) * INW + (c % 2) * HW + HW],
                            lambda c: w_in[(c // 2) * 128:(c // 2) * 128 + 128, (c % 2) * HW:(c % 2) * HW + HW],
                            16, HW, gain=gmix, gain_col=lambda c: c // 2, stage_ring=wst, eng_cycle=("act", "pool"))
                load_weight(tmp, Wkv, lambda c: Wkv[:, :], lambda c: w_kv, 1, 1024, gain=gkv,
                            gain_col=lambda c: 0, stage_ring=wst)
                load_weight(tmp, Wq, lambda c: Wq[:, c * 768:(c + 1) * 768],
                            lambda c: w_q[c * 128:(c + 1) * 128, :],
                            2, 768, gain=gq, gain_col=lambda c: c, stage_ring=wst)

                sc.barrier()
                if stop_after == 0.1:
                    dbg = nc.dram_tensor("dbg", [128, 1024], F32, kind="ExternalOutput").ap()
                    dma("sp", dbg, cos_all[:], R=[cos_all], ch=cos_all)
                    dbg2 = nc.dram_tensor("dbg2", [128, 1024], F32, kind="ExternalOutput").ap()
                    dma("sp", dbg2, sin_all[:], R=[sin_all], ch=sin_all)
                    dbg3 = nc.dram_tensor("dbg3", [128, 8 * INW], BF16, kind="ExternalOutput").ap()
                    dma("sp", dbg3, Win[:], R=[Win], ch=Win)
                    sc.barrier()
                    return nc

            xnTr = Ring(p1, "xnT", 2, [128, 8 * 512], BF16)
            stg = Ring(p1, "stg", 4, [128, 512], BF16)
            sqB = Ring(p1, "sqB", 2, [128, 256], BF16)
            ckvn = Ring(p1, "ckvn", 3, [128, 128], BF16)
            krr = Ring(p1, "krr", 3, [128, 128], BF16)
            for _b in krr.b:
                op("pool", lambda: nc.gpsimd.memset(_b[:], 0.0), W=[_b])
            rtk = Ring(p1, "rtk", 3, [128, 64], F32)
            ckvnT = Ring(p1, "ckvnT", 2, [128, 512], BF16)
            krT = Ring(p1, "krT", 2, [128, 512], BF16)
            cqn = Ring(p1, "cqn", 3, [128, 256], BF16)
            cqnT = Ring(p1, "cqnT", 3, [128, 256], BF16)
            qb = Ring(p1, "qb", 3, [128, 768], BF16)
            rtq = Ring(p1, "rtq", 4, [128, 2 * 128], F32)
            qbst = Ring(p1, "qbst", 2, [128, 1024], BF16)
            tmpf = Ring(p1, "tmpf", 2, [8, 512], F32)
            lTr = Ring(p1, "lTr", 2, [8, 512], F32)
            l_mid = Ring(p1, "l_mid", 2, [8, 512], BF16)
            l_lo = Ring(p1, "l_lo", 2, [8, 512], BF16)
            l_r = Ring(p1, "l_r", 2, [8, 512], F32)
            eights = sbt(p1, "eights", [8, 512], BF16)
            op("pool", lambda: nc.gpsimd.memset(eights[:], 8.0), W=[eights])
            ptx = Ring(p1, "ptx", 2, [128, 1024], BF16, psum=True)
            ppj = Ring(p1, "ppj", 2, [128, 512], F32, psum=True)
            pB = Ring(p1, "pB", 3, [128, 512], F32, psum=True)
            pBt = pst(p1, "pBt", [128, 1024], BF16)

            evac_flip = [0]
            evac_pat = [(1, 1, 1, 0)]
            lt_prev = [None]

            def evac(out_ap, in_ap, R, W):
                pat = evac_pat[0]
                evac_flip[0] = (evac_flip[0] + 1) % len(pat)
                if pat[evac_flip[0]]:
                    op("act", lambda: nc.scalar.copy(out=out_ap, in_=in_ap), R=R, W=W)
                else:
                    op("dve", lambda: nc.vector.tensor_copy(out=out_ap, in_=in_ap), R=R, W=W)

            def norm_block(src_ap, R_src, n, sq_buf, nparts=128):
                ss, ln_, rs = ssr.next(), lnr.next(), rsr.next()
                return ss, ln_, rs

            def prepB(xss):
                xn = xnTr.next()
                for j in range(4):
                    xs = xss[j]
                    pt = ptx.next()
                    for kc in range(8):
                        op("pe", lambda: nc.tensor.transpose(out=pt[:, kc * 128:(kc + 1) * 128],
                                                             in_=xs[:, kc * 128:(kc + 1) * 128], identity=ident[:]),
                           R=[xs, ident], W=[pt], acc=(kc > 0), signal=(kc == 7))
                    xv = xn[:].rearrange("p (c t) -> p c t", c=8)[:, :, j * 128:(j + 1) * 128]
                    pv = pt[:].rearrange("p (c t) -> p c t", c=8)
                    evac(xv, pv, [pt], [xn])
                return xn

            def proj_fm(xn, col0, ncols, N=512):
                ps = ppj.next()
                for kc in range(8):
                    op("pe", lambda: nc.tensor.matmul(ps[0:ncols, 0:N], lhsT=Win[:, kc * INW + col0:kc * INW + col0 + ncols],
                                                      rhs=xn[:, kc * 512:kc * 512 + N], start=(kc == 0), stop=(kc == 7)),
                       R=[Win, xn], W=[ps], acc=(kc > 0), signal=(kc == 7))
                return ps

            def proj_tm(xn, j, col0, ncols, ring):
                ps = ring.next()
                for kc in range(8):
                    op("pe", lambda: nc.tensor.matmul(ps[:, 0:ncols], lhsT=xn[:, kc * 512 + j * 128:kc * 512 + j * 128 + 128],
                                                      rhs=Win[:, kc * INW + col0:kc * INW + col0 + ncols],
                                                      start=(kc == 0), stop=(kc == 7)),
                       R=[Win, xn], W=[ps], acc=(kc > 0), signal=(kc == 7))
                return ps

            def small_norm(ps, c0, n, out_bf):
                sq = sqB.next()
                op("act", lambda: nc.scalar.activation(out=sq[:, 0:n], in_=ps[:, c0:c0 + n], func=AF.Square),
                   R=[ps], W=[sq])
                ss, ln_, rs = ssr.next(), lnr.next(), rsr.next()
                op("dve", lambda: nc.vector.tensor_reduce(out=ss[:], in_=sq[:, 0:n], axis=AX.X, op=ALU.add),
                   R=[sq], W=[ss])
                emit_rstd(ss, ln_, rs, 1, 1.0 / n)
                op("dve", lambda: nc.vector.tensor_scalar(out=out_bf[:, 0:n], in0=ps[:, c0:c0 + n],
                                                          scalar1=rs[:, 0:1], scalar2=None, op0=ALU.mult),
                   R=[ps, rs], W=[out_bf])

            pend_kv = []

            def proj_all(xn, T, hook1, hA, flush_kv=None):
                t0 = T * 512
                ckT = ckvnT.next()
                krt = krT.next()
                cns, krs = {}, {}

                def s1(j):
                    ps = proj_tm(xn, j, O_VA, 512, ppj)
                    st = stg.next()
                    evac(st[:], ps[:], [ps], [st])
                    dma("sp", VA[t0 + j * 128:t0 + j * 128 + 128, :], st[:], R=[st], ch=st)
                    pb = proj_tm(xn, j, O_CKV, 160, pB)
                    cn = ckvn.next()
                    small_norm(pb, 0, 128, cn)
                    blk = T * 4 + j
                    rt = rtk.next()
                    kr_ = krr.next()
                    cb = cos_all[:, blk * 16:blk * 16 + 16].unsqueeze(1).broadcast_to([128, 2, 16])
                    sb_ = sin_all[:, blk * 16:blk * 16 + 16].unsqueeze(1).broadcast_to([128, 2, 16])
                    kv3 = pb[:, 128:160].rearrange("p (a j) -> p a j", a=2)
                    op("dve", lambda: nc.vector.tensor_tensor(out=rt[:, 0:32].rearrange("p (a j) -> p a j", a=2),
                                                              in0=kv3, in1=cb, op=ALU.mult),
                       R=[pb, cos_all], W=[rt])
                    op("dve", lambda: nc.vector.tensor_tensor(out=rt[:, 32:64].rearrange("p (a j) -> p a j", a=2),
                                                              in0=kv3, in1=sb_, op=ALU.mult),
                       R=[pb, sin_all], W=[rt])
                    op("dve", lambda: nc.vector.tensor_tensor(out=kr_[:, 0:16], in0=rt[:, 0:16], in1=rt[:, 48:64],
                                                              op=ALU.subtract), R=[rt], W=[kr_])
                    op("dve", lambda: nc.vector.tensor_tensor(out=kr_[:, 16:32], in0=rt[:, 16:32], in1=rt[:, 32:48],
                                                              op=ALU.add), R=[rt], W=[kr_])
                    cns[j], krs[j] = cn, kr_

                def s2(j):
                    cn, kr_ = cns.pop(j), krs.pop(j)
                    op("pe", lambda: nc.tensor.transpose(out=pBt[:, 0:128], in_=cn[:, 0:128], identity=ident[:]),
                       R=[cn, ident], W=[pBt], signal=False)
                    op("pe", lambda: nc.tensor.transpose(out=pBt[:, 128:256], in_=kr_[:, 0:128], identity=ident[:]),
                       R=[kr_, ident], W=[pBt], acc=True)
                    pv2 = pBt[:, 0:256].rearrange("p (a t) -> p a t", a=2)
                    evac(ckT[:, j * 128:(j + 1) * 128], pBt[:, 0:128], [pBt], [ckT])
                    evac(krt[0:32, j * 128:(j + 1) * 128], pBt[0:32, 128:256], [pBt], [krt])

                def ka(jj):
                    ps = proj_fm(xn, O_KA + jj * 128, 128)
                    st = stg.next()
                    evac(st[:], ps[:], [ps], [st])
                    for hh_ in range(2):
                        dma("sp", KA[2 * jj + hh_, 0:64, t0:t0 + 512], st[hh_ * 64:(hh_ + 1) * 64, :], R=[st], ch=st)

                def forget():
                    ps = proj_fm(xn, O_FA, 8)
                    tf = tmpf.next()
                    op("act", lambda: nc.scalar.activation(out=tf[:], in_=ps[0:8, :], func=AF.Exp, bias=negb[:, 0:1],
                                                           scale=-1.0), R=[ps, negb], W=[tf])
                    lt = lTr.next()
                    op("act", lambda: nc.scalar.activation(out=lt[:], in_=tf[:], func=AF.Ln,
                                                           bias=oneb[0:8, 0:1], scale=1.0), R=[tf, oneb], W=[lt])
                    init = 0.0 if T == 0 else lt_prev[0][:, 511:512]
                    op("dve", lambda: nc.vector.tensor_tensor_scan(out=lt[:], data0=ones8[:], data1=lt[:], initial=init,
                                                                   op0=ALU.mult, op1=ALU.add),
                       R=[lt, ones8] + ([lt_prev[0]] if T > 0 else []), W=[lt])
                    lt_prev[0] = lt
                    lm, ll, lr = l_mid.next(), l_lo.next(), l_r.next()
                    lhs_ = LH[:, t0:t0 + 512]
                    op("dve", lambda: nc.vector.tensor_copy(out=lhs_, in_=lt[:]), R=[lt], W=[LH])
                    op("dve", lambda: nc.vector.tensor_tensor(out=lr[:], in0=lt[:], in1=lhs_, op=ALU.subtract),
                       R=[lt, LH], W=[lr])
                    op("dve", lambda: nc.vector.tensor_copy(out=lm[:], in_=lr[:]), R=[lr], W=[lm])
                    op("dve", lambda: nc.vector.tensor_tensor(out=lr[:], in0=lr[:], in1=lm[:], op=ALU.subtract),
                       R=[lr, lm], W=[lr])
                    op("dve", lambda: nc.vector.tensor_copy(out=ll[:], in_=lr[:]), R=[lr], W=[ll])
                    dma("sp", KA[:, 64, t0:t0 + 512], eights[:], R=[eights], ch=eights)
                    dma("sp", KA[:, 65, t0:t0 + 512], lhs_, R=[LH], ch=LH)
                    dma("sp", KA[:, 66, t0:t0 + 512], lm[:], R=[lm], ch=lm)
                    dma("sp", KA[:, 67, t0:t0 + 512], ll[:], R=[ll], ch=ll)

                def kvk(pT, pckT, jj):
                    pt0 = pT * 512
                    ps = ppj.next()
                    op("pe", lambda: nc.tensor.matmul(ps[:, :], lhsT=Wkv[:, jj * 128:(jj + 1) * 128], rhs=pckT[:, :],
                                                      start=True, stop=True), R=[Wkv, pckT], W=[ps])
                    st = stg.next()
                    evac(st[:], ps[:], [ps], [st])
                    dma("sp", KBN[2 * jj:2 * jj + 2, :, pt0:pt0 + 512].rearrange("h r t -> (h r) t"), st[:],
                        R=[st], ch=st)

                def vbk(pT, pckT, j):
                    pt0 = pT * 512
                    ps = ppj.next()
                    op("pe", lambda: nc.tensor.matmul(ps[:, :], lhsT=pckT[:, j * 128:(j + 1) * 128], rhs=Wkv[:, 512:1024],
                                                      start=True, stop=True), R=[Wkv, pckT], W=[ps])
                    st = stg.next()
                    evac(st[:], ps[:], [ps], [st])
                    dma("sp", VB[pt0 + j * 128:pt0 + j * 128 + 128, :], st[:], R=[st], ch=st)

                prev = pend_kv.pop() if pend_kv else None

                def pk(jj):
                    if prev is not None:
                        kvk(prev[0], prev[1], jj)

                def pv_(j):
                    if prev is not None:
                        vbk(prev[0], prev[1], j)

                s1(0)
                pk(0)
                s1(1)
                pk(1)
                s2(0)
                hook1()
                s1(2)
                pk(2)
                s2(1)
                s1(3)
                pk(3)
                s2(2)
                ka(0)
                pv_(0)
                hA[0]()
                ka(1)
                pv_(1)
                hA[1]()
                s2(3)
                dma("sp", KR[:, t0:t0 + 512], krt[0:32, :], R=[krt], ch=krt)
                ka(2)
                pv_(2)
                hA[2]()
                ka(3)
                pv_(3)
                hA[3]()
                forget()
                hA[4]()
                pend_kv.append((T, ckT))
                if flush_kv is not None:
                    for jj in range(4):
                        kvk(T, ckT, jj)
                    for j in range(4):
                        vbk(T, ckT, j)
                    pend_kv.pop()

            def proj_own(xn, T, hook1, hA):
                t0 = T * 512
                cqs, qbs = {}, {}

                def qa(jj):
                    ps = proj_fm(xn, O_QA + jj * 128, 128)
                    st = stg.next()
                    evac(st[:], ps[:], [ps], [st])
                    for hh_ in range(2):
                        dma("sp", QA[2 * jj + hh_, 0:64, t0:t0 + 512], st[hh_ * 64:(hh_ + 1) * 64, :], R=[st], ch=st)

                def gate(gc):
                    ps = proj_fm(xn, O_GA + gc * 128, 128)
                    st = stg.next()
                    op("act", lambda: nc.scalar.activation(out=st[:], in_=ps[:], func=AF.Sigmoid,
                                                           bias=bg[:, gc:gc + 1], scale=1.0), R=[ps, bg], W=[st])
                    for uu in range(2):
                        dma("sp", GT[2 * T + uu, :, gc, :], st[:, uu * 256:(uu + 1) * 256], R=[st], ch=st)

                def o1(j):
                    pb = proj_tm(xn, j, O_CQ, 256, pB)
                    cq_ = cqn.next()
                    small_norm(pb, 0, 256, cq_)
                    cqs[j] = cq_

                def o2(j):
                    blk = T * 4 + j
                    cq_ = cqs.pop(j)
                    cT = cqnT.next()
                    op("pe", lambda: nc.tensor.transpose(out=pBt[:, 0:128], in_=cq_[:, 0:128], identity=ident[:]),
                       R=[cq_, ident], W=[pBt], signal=False)
                    op("pe", lambda: nc.tensor.transpose(out=pBt[:, 128:256], in_=cq_[:, 128:256], identity=ident[:]),
                       R=[cq_, ident], W=[pBt], acc=True)
                    evac(cT[:, 0:256], pBt[:, 0:256], [pBt], [cT])
                    qb_ = qb.next()
                    q3 = qb_[:].rearrange("p (h c) -> p h c", h=8)
                    for half in range(2):
                        pq = pB.next()
                        for lc in range(2):
                            op("pe", lambda: nc.tensor.matmul(pq[:, 0:384], lhsT=cT[:, lc * 128:(lc + 1) * 128],
                                                              rhs=Wq[:, lc * 768 + half * 384:lc * 768 + half * 384 + 384],
                                                              start=(lc == 0), stop=(lc == 1)),
                               R=[Wq, cT], W=[pq], acc=(lc > 0), signal=(lc == 1))
                        p3 = pq[:, 0:384].rearrange("p (h c) -> p h c", h=4)
                        qh = q3[:, half * 4:(half + 1) * 4, :]
                        rt = rtq.next()
                        r4a = rt[:, 0:128].rearrange("p (h a j) -> p h a j", h=4, a=2)
                        r4b = rt[:, 128:256].rearrange("p (h a j) -> p h a j", h=4, a=2)
                        pr4 = p3[:, :, 64:96].rearrange("p h (a j) -> p h a j", a=2)
                        cb = cos_own[:, blk * 16:blk * 16 + 16].unsqueeze(1).unsqueeze(1).broadcast_to([128, 4, 2, 16])
                        sb_ = sin_own[:, blk * 16:blk * 16 + 16].unsqueeze(1).unsqueeze(1).broadcast_to([128, 4, 2, 16])
                        op("dve", lambda: nc.vector.tensor_copy(out=qh[:, :, 0:64], in_=p3[:, :, 0:64]), R=[pq], W=[qb_])
                        op("dve", lambda: nc.vector.tensor_tensor(out=r4a, in0=pr4, in1=cb, op=ALU.mult),
                           R=[pq, cos_own], W=[rt])
                        op("dve", lambda: nc.vector.tensor_tensor(out=r4b, in0=pr4, in1=sb_, op=ALU.mult),
                           R=[pq, sin_own], W=[rt])
                        op("dve", lambda: nc.vector.tensor_tensor(out=qh[:, :, 64:80], in0=r4a[:, :, 0, :],
                                                                  in1=r4b[:, :, 1, :], op=ALU.subtract),
                           R=[rt], W=[qb_])
                        op("dve", lambda: nc.vector.tensor_tensor(out=qh[:, :, 80:96], in0=r4a[:, :, 1, :],
                                                                  in1=r4b[:, :, 0, :], op=ALU.add),
                           R=[rt], W=[qb_])
                    qbs[j] = qb_

                def o3(j):
                    qb_ = qbs.pop(j)
                    for h in range(8):
                        op("pe", lambda: nc.tensor.transpose(out=pBt[0:96, h * 128:(h + 1) * 128],
                                                             in_=qb_[:, h * 96:(h + 1) * 96], identity=ident[:]),
                           R=[qb_, ident], W=[pBt], acc=(h > 0), signal=(h == 7))
                    qs = qbst.next()
                    evac(qs[0:96, :], pBt[0:96, :], [pBt], [qs])
                    dma("sp", QB[:, :, t0 + j * 128:t0 + j * 128 + 128].rearrange("h r t -> r h t"),
                        qs[0:96, :].rearrange("p (h t) -> p h t", h=8), R=[qs], ch=qs)

                for r_ in range(3):
                    dma("sp", QA[:, 65 + r_, t0:t0 + 512], eights[:], R=[eights], ch=eights)
                sched = {0: [lambda: o1(0)], 1: [lambda: o1(1)], 2: [lambda: o2(0)], 3: [hook1, lambda: o1(2)],
                         4: [lambda: o2(1)], 5: [lambda: o3(0)], 6: [lambda: o1(3)], 7: [lambda: o2(2)],
                         8: [lambda: o3(1)], 9: [lambda: o2(3)], 10: [lambda: o3(2)], 11: [hA[0]],
                         12: [lambda: o3(3)], 13: [hA[1]], 14: [hA[2]], 15: [hA[3]], 17: [hA[4]]}
                for s in range(20):
                    if s < 16:
                        gate(s)
                    else:
                        qa(s - 16)
                    for f in sched.get(s, []):
                        f()


            xnB = {0: prepB(xsA.pop(0))}
            if NJ > 1:
                xsA[1] = prepA(xL.pop(1))
            if NJ > 2:
                xL[2] = do_prepL(jobs[2])
            for n, job in enumerate(jobs):
                def hook1(n=n):
                    if n + 1 < NJ:
                        xnB[n + 1] = prepB(xsA.pop(n + 1))

                st3s = {}

                def mk_sq(j, n=n):
                    def f():
                        if n + 2 < NJ:
                            if j == 0:
                                st3s[n] = (ss4r.next(), ln4r.next(), rs4r.next())
                            prepA_sq(xL[n + 2], j, st3s[n])
                    return f

                def fin(n=n):
                    if n + 2 < NJ:
                        xsA[n + 2] = prepA_fin(xL.pop(n + 2), st3s.pop(n))
                    if n + 3 < NJ:
                        xL[n + 3] = do_prepL(jobs[n + 3])

                hA = [mk_sq(0), mk_sq(1), mk_sq(2), mk_sq(3), fin]

                cur = xnB.pop(n)
                evac_pat[0] = (1, 1, 1, 0) if job[0] == "all" else (0, 0, 1)
                if job[0] == "all":
                    last_all = (n + 1 >= NJ) or (jobs[n + 1][0] != "all")
                    proj_all(cur, job[1], hook1, hA, flush_kv=(True if last_all else None))
                else:
                    proj_own(cur, job[1], hook1, hA)

            qaug = sbt(p1, "qaug", [8, SO], BF16)
            tmpb = sbt(p1, "tmpb", [8, 8 * 128], BF16)
            l4 = LH[:].rearrange("h (g b t) -> h g b t", g=8, b=8)
            qa4 = qaug[:].rearrange("h (g p t) -> h g p t", g=8, p=4)
            tb3 = tmpb[:].rearrange("h (g t) -> h g t", g=8)
            for p in range(4):
                op("dve", lambda: nc.vector.tensor_scalar(out=tb3, in0=l4[:, :, 2 * p, :], scalar1=sel[:, p:p + 1],
                                                          scalar2=None, op0=ALU.mult), R=[LH, sel], W=[tmpb])
                op("dve", lambda: nc.vector.scalar_tensor_tensor(out=qa4[:, :, p, :], in0=l4[:, :, 2 * p + 1, :],
                                                                 scalar=sel[:, 4 + p:5 + p], in1=tb3,
                                                                 op0=ALU.mult, op1=ALU.add),
                   R=[LH, sel, tmpb], W=[qaug])
            dma("sp", QA[:, 64, :], qaug[:], R=[qaug], ch=qaug)
            sc.barrier()

        if stop_after <= 1:
            return nc
        Wg = sbt(top, "Wg", [128, 8 * FF], BF16, side="right")
        w3a = ExitStack()
        Wa = sbt(w3a, "Wa", [128, 4 * D], BF16)
        Wb = sbt(w3a, "Wb", [128, 4 * D], BF16)
        Wo = sbt(w3a, "Wo", [128, 8 * D], BF16)
        with ExitStack() as p2:
            mask_a = sbt(p2, "mask_a", [128, 8 * 512], BF16)
            mask_b = sbt(p2, "mask_b", [128, 8 * 512], BF16)
            KT = [sbt(p2, "KT%d" % i, [128, S], BF16) for i in range(2)]
            VV = [sbt(p2, "VV%d" % i, [128, 64 * 128], BF16) for i in range(2)]
            for i in range(2):
                v3 = VV[i][:].rearrange("p (k c) -> p k c", c=128)
                op("pool", lambda: nc.gpsimd.memset(v3[:, :, 64:128], 1.0), W=[VV[i]])
            QT = Ring(p2, "QT", 4, [128, 512], BF16)
            wg_prefetch = [False]
            PT = Ring(p2, "PT", 4, [128, 1024], BF16)
            rden = Ring(p2, "rden", 2, [64, 512], F32)
            yst = Ring(p2, "yst", 2, [64, 512], BF16)
            pS = Ring(p2, "pS", 3, [128, 1024], F32, psum=True)
            pO = Ring(p2, "pO", 2, [128, 512], F32, psum=True)

            heads = [(1, h) for h in range(NH)] + [(0, h) for h in range(NH)]
            NI = 8
            if stop_after == 2.5:
                heads = heads[:1] + heads[8:9]
            tiles = [(n, i) for n in range(len(heads)) for i in range(NI)]
            steps = [(ti, kb) for ti, (n, i) in enumerate(tiles) for kb in range(0, 8 * i + 8, 2)]

            def load_head(n):
                m, h = heads[n]
                kt, vv = KT[n % 2], VV[n % 2]
                v3 = vv[:].rearrange("p (k c) -> p k c", c=128)
                if m == 1:
                    dma("sp", kt[0:64, :], KBN[h], W=[kt], ch=kt)
                    dma("sp", kt[64:96, :], KR, W=[kt], ch=kt, group=True)
                    vsrc = VB
                else:
                    dma("sp", kt[0:64, :], KA[h, 0:64, :], W=[kt], ch=kt)
                    dma("sp", kt[64:68, :], KA[h, 64:68, :], W=[kt], ch=kt, group=True)
                    vsrc = VA
                vs3 = vsrc[:, h * 64:(h + 1) * 64].rearrange("(k p) d -> p k d", p=128)
                for q4 in range(4):
                    dma("sp", v3[:, q4 * 16:(q4 + 1) * 16, 0:64], vs3[:, q4 * 16:(q4 + 1) * 16, :], W=[vv], ch=vv,
                        group=(q4 > 0))

            qt_of = {}

            def load_q(ti):
                n, i = tiles[ti]
                m, h = heads[n]
                qt = QT.next()
                qt_of[ti] = qt
                if m == 1:
                    dma("sp", qt[0:96, :], QB[h, :, i * 512:(i + 1) * 512], W=[qt], ch=qt)
                else:
                    dma("sp", qt[0:68, :], QA[h, :, i * 512:(i + 1) * 512], W=[qt], ch=qt)

            st_S = {}
            st_P = {}
            po_of = {}

            def emit_qk(idx):
                ti, kb0 = steps[idx]
                n, i = tiles[ti]
                m, h = heads[n]
                kt = KT[n % 2]
                qt = qt_of[ti]
                kc = 96 if m == 1 else 68
                d0 = kb0 - 8 * i
                n0 = 0 if d0 < 0 else 128 * (d0 // 2)
                ps = pS.next()
                for e in range(2):
                    kb = kb0 + e
                    d = kb - 8 * i
                    pse = ps[:, e * 512 + n0:e * 512 + 512]
                    if d < 0:
                        op("pe", lambda: nc.tensor.matmul(pse, lhsT=kt[0:kc, kb * 128:(kb + 1) * 128],
                                                          rhs=qt[0:kc, n0:512], start=True, stop=True),
                           R=[kt, qt], W=[ps], acc=(e == 1), signal=(e == 1))
                    else:
                        mk = mask_b if m == 1 else mask_a
                        op("pe", lambda: nc.tensor.matmul(pse, lhsT=kt[0:kc, kb * 128:(kb + 1) * 128],
                                                          rhs=qt[0:kc, n0:512], start=True, stop=False),
                           R=[kt, qt], W=[ps], acc=(e == 1), signal=False)
                        op("pe", lambda: nc.tensor.matmul(ps[:, e * 512 + n0:e * 512 + n0 + 128], lhsT=ident[:, :],
                                                          rhs=mk[:, d * 512 + n0:d * 512 + n0 + 128], start=False, stop=True),
                           R=[ident, mk], W=[ps], acc=True, signal=(e == 1))
                pt = PT.next()
                ps3 = ps[:].rearrange("p (e q) -> p e q", e=2)[:, :, n0:512]
                pt3 = pt[:].rearrange("p (e q) -> p e q", e=2)[:, :, n0:512]
                op("act", lambda: nc.scalar.activation(out=pt3, in_=ps3, func=AF.Exp,
                                                       scale=(SC_B if m == 1 else SC_A)), R=[ps], W=[pt])
                st_P[idx] = (pt, n0)

            def emit_pv(idx):
                ti, kb0 = steps[idx]
                n, i = tiles[ti]
                m, h = heads[n]
                nkb = 8 * i + 8
                vv = VV[n % 2]
                pt, n0 = st_P.pop(idx)
                if kb0 == 0:
                    po_of[ti] = pO.next()
                    if ti + 3 < len(tiles):
                        load_q(ti + 3)
                    if i == 0 and n + 1 < len(heads):
                        load_head(n + 1)
                    if i == 1 and n == 2:
                        load_weight_cast(Wg, lambda c: Wg[:, c * FF:(c + 1) * FF],
                                         lambda c: w_g[c * 128:(c + 1) * 128, :], 8)
                    if i == 1 and not wg_prefetch[0]:
                        wg_prefetch[0] = True
                        load_weight_cast(Wa, lambda c: Wa[:, c * D:(c + 1) * D], lambda c: w_a[c * 128:(c + 1) * 128, :], 4)
                        load_weight_cast(Wb, lambda c: Wb[:, c * D:(c + 1) * D], lambda c: w_b[c * 128:(c + 1) * 128, :], 4)
                        load_weight_cast(Wo, lambda c: Wo[:, c * D:(c + 1) * D], lambda c: w_o[c * 128:(c + 1) * 128, :], 8)
                po = po_of[ti]
                for e in range(2):
                    kb = kb0 + e
                    last = (kb == nkb - 1)
                    op("pe", lambda: nc.tensor.matmul(po[:, n0:512], lhsT=vv[:, kb * 128:(kb + 1) * 128],
                                                      rhs=pt[:, e * 512 + n0:e * 512 + 512],
                                                      start=(kb == 0), stop=last),
                       R=[vv, pt], W=[po], acc=(kb > 0), signal=(e == 1))
                if last:
                    rd = rden.next()
                    ys = yst.next()
                    op("dve", lambda: nc.vector.reciprocal(out=rd[0:64, :], in_=po[64:128, :]), R=[po], W=[rd])
                    op("dve", lambda: nc.vector.tensor_tensor(out=ys[0:64, :], in0=po[0:64, :], in1=rd[0:64, :],
                                                              op=ALU.mult), R=[po, rd], W=[ys])
                    r0 = (0 if m == 0 else 512) + h * 64
                    for uu in range(2):
                        dma("pool", YT[2 * i + uu, (r0 % 128):(r0 % 128) + 64, r0 // 128, :],
                            ys[0:64, uu * 256:(uu + 1) * 256], R=[ys], ch=ys)
                    del po_of[ti]

            load_head(0)
            dma("pool", mask_b[:], mask_b_in, W=[mask_b], ch=mask_b)
            dma("pool", mask_a[:], mask_a_in, W=[mask_a], ch=mask_a)
            load_q(0)
            load_q(1)
            load_q(2)
            LA = 2
            for idx in range(len(steps) + LA):
                if idx < len(steps):
                    emit_qk(idx)
                if idx - LA >= 0:
                    emit_pv(idx - LA)
            sc.barrier()

        if stop_after <= 2.5:
            return nc

        Wu = sbt(top, "Wu", [128, 8 * FF], BF16, side="right")
        with ExitStack() as p3:
            gffn_b = sbt(p3, "gffn_b", [128, D], F32)
            dma("sp", gffn_b[:], g_ffn_row.partition_broadcast(128), W=[gffn_b], ch=gffn_b)
            T3 = 256
            yT = Ring(p3, "yT", 2, [128, 8 * T3], BF16)
            gT = Ring(p3, "gT", 2, [128, 16 * T3], BF16)
            xblk = Ring(p3, "xb3", 5, [128, D], F32)
            t1r = Ring(p3, "t1r", 2, [128, T3], F32)
            t2r = Ring(p3, "t2r", 2, [128, T3], F32)
            mT = Ring(p3, "mT", 2, [128, 8 * T3], BF16)
            sqr = Ring(p3, "sq3", 2, [128, D], BF16)
            hsr = Ring(p3, "hs3", 4, [128, D], BF16)
            ssr = Ring(p3, "ss3", 4, [128, 1], F32)
            lnr = Ring(p3, "ln3", 4, [128, 1], F32)
            rsr = Ring(p3, "rs3", 4, [128, 1], F32)
            hnT = Ring(p3, "hnT3", 2, [128, 8 * T3], BF16)
            pa_r = Ring(p3, "pa", 2, [128, 512], F32, psum=True)
            pb_r = Ring(p3, "pb", 2, [128, 512], F32, psum=True)
            po_r = Ring(p3, "po3", 2, [128, 512], F32, psum=True)
            ptx = Ring(p3, "ptx3", 2, [128, 1024], BF16, psum=True)
            NB3 = T3 // 128

            def load_tile3(i):
                y = yT.next()
                g = gT.next()
                dma("sp", y[:], YT[i].rearrange("p c t -> p (c t)"), W=[y], ch=y)
                dma("sp", g[:], GT[i].rearrange("p c t -> p (c t)"), W=[g], ch=g)
                return y, g

            pend = []

            def flush_one():
                hs, hn, j, ti = pend.pop(0)
                pt = ptx.next()
                for kc in range(8):
                    op("pe", lambda: nc.tensor.transpose(out=pt[:, kc * 128:(kc + 1) * 128],
                                                         in_=hs[:, kc * 128:(kc + 1) * 128], identity=ident[:]),
                       R=[hs, ident], W=[pt], acc=(kc > 0), signal=(kc == 7))
                op("act", lambda: nc.scalar.copy(
                    out=hn[:].rearrange("p (c t) -> p c t", c=8)[:, :, j * 128:(j + 1) * 128],
                    in_=pt[:].rearrange("p (c t) -> p c t", c=8)), R=[pt], W=[hn])
                if j == NB3 - 1:
                    dma("sp", HNT[ti], hn[:], R=[hn], ch=hn)

            NT3 = SO // T3

            def branch(i, y, g):
                m_ = mT.next()
                for mc in range(8):
                    pa, pb = pa_r.next(), pb_r.next()
                    for kc in range(4):
                        op("pe", lambda: nc.tensor.matmul(pa[:, 0:T3], lhsT=Wa[:, kc * D + mc * 128:kc * D + mc * 128 + 128],
                                                          rhs=y[:, kc * T3:(kc + 1) * T3], start=(kc == 0), stop=(kc == 3)),
                           R=[Wa, y], W=[pa], acc=(kc > 0), signal=(kc == 3))
                    for kc in range(4):
                        op("pe", lambda: nc.tensor.matmul(pb[:, 0:T3], lhsT=Wb[:, kc * D + mc * 128:kc * D + mc * 128 + 128],
                                                          rhs=y[:, (4 + kc) * T3:(5 + kc) * T3], start=(kc == 0),
                                                          stop=(kc == 3)),
                           R=[Wb, y], W=[pb], acc=(kc > 0), signal=(kc == 3))
                    t1, t2 = t1r.next(), t2r.next()
                    op("dve", lambda: nc.vector.tensor_tensor(out=t1[:], in0=pa[:, 0:T3], in1=g[:, mc * T3:(mc + 1) * T3],
                                                              op=ALU.mult), R=[pa, g], W=[t1])
                    op("dve", lambda: nc.vector.tensor_tensor(out=t2[:], in0=pb[:, 0:T3],
                                                              in1=g[:, (8 + mc) * T3:(9 + mc) * T3], op=ALU.mult),
                       R=[pb, g], W=[t2])
                    op("pool", lambda: nc.gpsimd.tensor_tensor(out=m_[:, mc * T3:(mc + 1) * T3], in0=t1[:], in1=t2[:],
                                                               op=ALU.add), R=[t1, t2], W=[m_])
                    if mc in (1, 5) and pend:
                        flush_one()
                return m_

            def load_x3(i):
                xbs = []
                for j in range(NB3):
                    t0 = i * T3 + j * 128
                    xb = xblk.next()
                    dma("sp", xb[:], x_own[t0:t0 + 128, :], W=[xb], ch=xb)
                    xbs.append(xb)
                return xbs

            def outproj(i, m_, xbs):
                hn = hnT.next()
                for j in range(NB3):
                    t0 = i * T3 + j * 128
                    xb = xbs[j]
                    for hf in range(2):
                        po = po_r.next()
                        for kc in range(8):
                            op("pe", lambda: nc.tensor.matmul(po[:, :], lhsT=m_[:, kc * T3 + j * 128:kc * T3 + j * 128 + 128],
                                                              rhs=Wo[:, kc * D + hf * 512:kc * D + hf * 512 + 512],
                                                              start=(kc == 0), stop=(kc == 7)),
                               R=[Wo, m_], W=[po], acc=(kc > 0), signal=(kc == 7))
                        op("dve", lambda: nc.vector.tensor_tensor(out=xb[:, hf * 512:(hf + 1) * 512], in0=po[:],
                                                                  in1=xb[:, hf * 512:(hf + 1) * 512], op=ALU.add),
                           R=[po, xb], W=[xb])
                    dma("sp", HH[t0:t0 + 128, :], xb[:], R=[xb], ch=xb)
                    sq = sqr.next()
                    op("act", lambda: nc.scalar.activation(out=sq[:], in_=xb[:], func=AF.Square), R=[xb], W=[sq])
                    ss, ln_, rs = ssr.next(), lnr.next(), rsr.next()
                    op("dve", lambda: nc.vector.tensor_reduce(out=ss[:], in_=sq[:], axis=AX.X, op=ALU.add),
                       R=[sq], W=[ss])
                    emit_rstd(ss, ln_, rs, 1, 1.0 / D)
                    hs = hsr.next()
                    op("dve", lambda: nc.vector.scalar_tensor_tensor(out=hs[:], in0=xb[:], scalar=rs[:, 0:1],
                                                                     in1=gffn_b[:], op0=ALU.mult, op1=ALU.mult),
                       R=[xb, rs, gffn_b], W=[hs])
                    pend.append((hs, hn, j, i))

            tl = {0: load_tile3(0)}
            xl = {0: load_x3(0)}
            if NT3 > 1:
                tl[1] = load_tile3(1)
            ms = {0: branch(0, *tl[0])}
            for i in range(NT3):
                if i % 2 == 0:
                    c_ = i // 2
                    dma("pool", Wu[:, c_ * FF:(c_ + 1) * FF], w_u[c_ * 128:(c_ + 1) * 128, :], W=[Wu], ch=Wu,
                        group=(c_ > 0), max_dma_last_dim=2048)
                if i + 1 < NT3:
                    xl[i + 1] = load_x3(i + 1)
                    ms[i + 1] = branch(i + 1, *tl.pop(i + 1))
                    tl.pop(i, None)
                    if i + 2 < NT3:
                        tl[i + 2] = load_tile3(i + 2)
                outproj(i, ms.pop(i), xl.pop(i))
            while pend:
                flush_one()
            sc.barrier()
        w3a.close()

        if stop_after <= 3:
            return nc

        with ExitStack() as p4:
            Wd = sbt(p4, "Wd", [128, NFC * D], BF16)
            gfin = sbt(p4, "gfin", [128, D], F32)
            dma("sp", gfin[:], g_fin.partition_broadcast(128), W=[gfin], ch=gfin)
            TW = 256
            hnr = Ring(p4, "hn4", 2, [128, 8 * TW], BF16)
            hbr = Ring(p4, "hb4", 4, [128, D], F32)
            sgr = Ring(p4, "sg4", 2, [128, TW], F32)
            actT = Ring(p4, "actT", 2, [128, NFC * TW], BF16)
            sqr = Ring(p4, "sq4", 2, [128, D], BF16)
            yor = Ring(p4, "yo4", 2, [128, D], F32)
            ssr = Ring(p4, "ss4", 4, [128, 1], F32)
            lnr = Ring(p4, "ln4", 4, [128, 1], F32)
            rsr = Ring(p4, "rs4", 4, [128, 1], F32)
            pg_r = Ring(p4, "pg", 2, [128, 512], F32, psum=True)
            pu_r = Ring(p4, "pu", 2, [128, 512], F32, psum=True)
            po_r = Ring(p4, "po4", 4, [128, 512], F32, psum=True)
            NT4 = SO // TW

            def load_tile4(u):
                hn = hnr.next()
                dma("sp", hn[:], HNT[u], W=[hn], ch=hn)
                hbs = []
                for j in range(TW // 128):
                    hb = hbr.next()
                    t0 = u * TW + j * 128
                    dma("sp", hb[:], HH[t0:t0 + 128, :], W=[hb], ch=hb)
                    hbs.append(hb)
                return hn, hbs

            nxt4 = load_tile4(0)
            load_weight_cast(Wd, lambda c: Wd[:, c * D:(c + 1) * D], lambda c: w_d[c * 128:(c + 1) * 128, :], NFC)
            for u in range(NT4):
                hn, hbs = nxt4
                if u + 1 < NT4:
                    nxt4 = load_tile4(u + 1)
                at = actT.next()
                for f in range(NFC):
                    pg, pu = pg_r.next(), pu_r.next()
                    for kc in range(8):
                        op("pe", lambda: nc.tensor.matmul(pg[:, 0:TW], lhsT=Wg[:, kc * FF + f * 128:kc * FF + f * 128 + 128],
                                                          rhs=hn[:, kc * TW:(kc + 1) * TW], start=(kc == 0), stop=(kc == 7)),
                           R=[Wg, hn], W=[pg], acc=(kc > 0), signal=(kc == 7))
                    for kc in range(8):
                        op("pe", lambda: nc.tensor.matmul(pu[:, 0:TW], lhsT=Wu[:, kc * FF + f * 128:kc * FF + f * 128 + 128],
                                                          rhs=hn[:, kc * TW:(kc + 1) * TW], start=(kc == 0), stop=(kc == 7)),
                           R=[Wu, hn], W=[pu], acc=(kc > 0), signal=(kc == 7))
                    sg = sgr.next()
                    op("act", lambda: nc.scalar.activation(out=sg[:], in_=pg[:, 0:TW], func=AF.Silu), R=[pg], W=[sg])
                    op("dve", lambda: nc.vector.tensor_tensor(out=at[:, f * TW:(f + 1) * TW], in0=pu[:, 0:TW], in1=sg[:],
                                                              op=ALU.mult), R=[pu, sg], W=[at])
                for j in range(TW // 128):
                    hb = hbs[j]
                    t0 = u * TW + j * 128
                    for hf in range(2):
                        po = po_r.next()
                        for f in range(NFC):
                            op("pe", lambda: nc.tensor.matmul(po[:, :], lhsT=at[:, f * TW + j * 128:f * TW + j * 128 + 128],
                                                              rhs=Wd[:, f * D + hf * 512:f * D + hf * 512 + 512],
                                                              start=(f == 0), stop=(f == NFC - 1)),
                               R=[Wd, at], W=[po], acc=(f > 0), signal=(f == NFC - 1))
                        op("dve", lambda: nc.vector.tensor_tensor(out=hb[:, hf * 512:(hf + 1) * 512], in0=po[:],
                                                                  in1=hb[:, hf * 512:(hf + 1) * 512], op=ALU.add),
                           R=[po, hb], W=[hb])
                    sq = sqr.next()
                    op("pool", lambda: nc.gpsimd.tensor_tensor(out=sq[:], in0=hb[:], in1=hb[:], op=ALU.mult),
                       R=[hb], W=[sq])
                    ss, ln_, rs = ssr.next(), lnr.next(), rsr.next()
                    op("dve", lambda: nc.vector.tensor_reduce(out=ss[:], in_=sq[:], axis=AX.X, op=ALU.add),
                       R=[sq], W=[ss])
                    emit_rstd(ss, ln_, rs, 1, 1.0 / D)
                    yo = yor.next()
                    op("dve", lambda: nc.vector.scalar_tensor_tensor(out=yo[:], in0=hb[:], scalar=rs[:, 0:1], in1=gfin[:],
                                                                     op0=ALU.mult, op1=ALU.mult),
                       R=[hb, rs, gfin], W=[yo])
                    dma("sp", out_own[t0:t0 + 128, :], yo[:], R=[yo], ch=yo)
            sc.barrier()
    return nc


def own_token_index(hh):
    idx = []
    for g in range(8):
        for p in range(4):
            b0 = (g * 8 + OWN[hh][p]) * 128
            idx.append(np.arange(b0, b0 + 128))
    return np.concatenate(idx)


def make_masks(hh):
    ma = np.zeros((128, 8, 512), np.float32)
    mb = np.zeros((128, 8, 512), np.float32)
    k = np.arange(128)[:, None]
    for kb in range(8):
        kpos = kb * 128 + k
        for p in range(4):
            qpos = OWN[hh][p] * 128 + np.arange(128)[None, :]
            ma[:, kb, p * 128:(p + 1) * 128] = np.where(kpos <= qpos, 0.0, NEG)
            mb[:, kb, p * 128:(p + 1) * 128] = np.where(kpos // 64 <= qpos // 64, 0.0, NEG)
    return ma.reshape(128, 4096), mb.reshape(128, 4096)


def prep_core_inputs(c, inp, shared):
    b, hh = c // 2, c % 2
    own = own_token_index(hh)
    xb = np.asarray(inp["x"][b], dtype=np.float32)
    pos = np.asarray(inp["positions"][b]).astype(np.int32)
    ma, mb = make_masks(hh)
    s = np.array([1.0 if OWN[hh][p] == 2 * p else 0.0 for p in range(4)], np.float32)
    sel = np.tile(np.concatenate([-s, -(1.0 - s)])[None, :], (8, 1)).astype(np.float32)
    d = dict(shared)
    d.update({
        "x_all": np.ascontiguousarray(xb),
        "x_own": np.ascontiguousarray(xb[own]),
        "pos_all": np.ascontiguousarray(pos.reshape(64, 128).T),
        "pos_own": np.ascontiguousarray(pos[own].reshape(32, 128).T),
        "mask_a": ma, "mask_b": mb, "sel": sel,
    })
    return d


def prep_shared(inp):
    f = lambda a: np.ascontiguousarray(np.asarray(a, dtype=np.float32))
    wkv = f(inp["w_kv_up"][0]).reshape(128, 8, 2, 64)
    wkv = np.concatenate([wkv[:, :, 0, :].reshape(128, 512), wkv[:, :, 1, :].reshape(128, 512)], axis=1)
    half = 16
    invf = (np.float32(10000.0) ** (-np.arange(half, dtype=np.float32) / np.float32(half))).astype(np.float32)
    return {
        "w_in": f(inp["w_in"][0]),
        "w_q": f(inp["w_q_up"][0]),
        "w_kv": np.ascontiguousarray(wkv),
        "w_a": f(inp["w_branch_a"][0]),
        "w_b": f(inp["w_branch_b"][0]),
        "w_o": f(inp["w_out"][0]),
        "w_g": f(inp["w_ffn_gate"][0]),
        "w_u": f(inp["w_ffn_up"][0]),
        "w_d": f(inp["w_ffn_down"][0]),
        "g_mix": np.ascontiguousarray(f(inp["norm_mix_g"][0]).reshape(8, 128).T),
        "g_ffn": np.ascontiguousarray(f(inp["norm_ffn_g"][0]).reshape(8, 128).T),
        "g_q": np.ascontiguousarray(f(inp["q_a_norm_g"][0]).reshape(2, 128).T),
        "g_kv": np.ascontiguousarray(f(inp["kv_a_norm_g"][0]).reshape(1, 128).T),
        "g_fin": f(inp["norm_final_g"]).reshape(1, D),
        "g_ffn_row": f(inp["norm_ffn_g"][0]).reshape(1, D),
        "b_f": f(inp["b_forget"][0]).reshape(8, 1),
        "b_gate": np.ascontiguousarray(f(inp["b_gate"][0]).reshape(16, 128).T),
        "ident": np.eye(128, dtype=np.float32),
        "invf": np.tile(invf[None, :], (128, 1)).astype(np.float32),
    }


_NC_CACHE = {}


def kernel(**inputs):
    if "nc" not in _NC_CACHE:
        _NC_CACHE["nc"] = build_program()
    nc = _NC_CACHE["nc"]
    shared = prep_shared(inputs)
    in_maps = [prep_core_inputs(c, inputs, shared) for c in range(8)]
    res = run_bass_kernel_spmd(nc, in_maps, core_ids=list(range(8)))
    out = np.empty((4, S, D), np.float32)
    for c in range(8):
        b, hh = c // 2, c % 2
        out[b, own_token_index(hh)] = np.asarray(res.results[c]["out_own"], dtype=np.float32)
    return out
```
